# Optimizing a Trainium2 kernel written in Bass

```python
import jax, jax.numpy as jnp
from jax import lax
import numpy as np

D_MODEL = 1024
BATCH = 16
SEQ = 2048
DEPTH = 2

GRID_W = 64
ROPE_THETA = 10000.0
EPS = 1e-6
HEAD_DIM = 64
BLOCK = 128
WINDOW = 128

A_HEADS = 6
A_KV_HEADS = 2
B_HEADS = 6
B_KV_HEADS = 2
C_HEADS = 4
C_NOPE = 64
C_ROPE = 32
C_V = 64
C_Q_RANK = 192
C_KV_RANK = 128

A_W = A_HEADS * HEAD_DIM
A_KV_W = A_KV_HEADS * HEAD_DIM
B_W = B_HEADS * HEAD_DIM
B_KV_W = B_KV_HEADS * HEAD_DIM
C_W = C_HEADS * C_V
MIX_W = A_W + B_W + C_W

IN_SPLITS = (A_W, A_KV_W, A_KV_W, A_W,
             B_W, B_KV_W, B_KV_W, B_W,
             C_Q_RANK, C_KV_RANK, C_ROPE, C_W)
IN_COLS = 2656

kernel_name = "hybrid_parallel_heads_encoder"


def rms_norm(x, g):
    xf = x.astype(jnp.float32)
    y = xf * lax.rsqrt(jnp.mean(xf * xf, axis=-1, keepdims=True) + EPS)
    return (y * g.astype(jnp.float32)).astype(x.dtype)


def rope_tables(pos, dim):
    inv = ROPE_THETA ** (-jnp.arange(0, dim, 2, dtype=jnp.float32) / dim)
    ang = pos.astype(jnp.float32)[:, None] * inv[None, :]
    ang = jnp.concatenate([ang, ang], axis=-1)
    return jnp.cos(ang), jnp.sin(ang)


def apply_rope(x, cos, sin):
    x1, x2 = jnp.split(x, 2, axis=-1)
    rot = jnp.concatenate([-x2, x1], axis=-1)
    c = cos[None, :, None, :].astype(x.dtype)
    s = sin[None, :, None, :].astype(x.dtype)
    return x * c + rot * s


def apply_axial_rope(x, row_cs, col_cs):
    half = x.shape[-1] // 2
    xr = apply_rope(x[..., :half], row_cs[0], row_cs[1])
    xc = apply_rope(x[..., half:], col_cs[0], col_cs[1])
    return jnp.concatenate([xr, xc], axis=-1)


def dense_gqa_blocks(q, k, v, scale):
    bsz, s_len, n_h, d = q.shape
    n_kv = k.shape[2]
    grp = n_h // n_kv
    nb = s_len // BLOCK
    qb = q.reshape(bsz, nb, BLOCK, n_kv, grp, d).transpose(1, 0, 2, 3, 4, 5)

    def one_block(q_blk):
        s = jnp.einsum('bqhgd,bkhd->bhgqk', q_blk, k).astype(jnp.float32) * scale
        p = jax.nn.softmax(s, axis=-1).astype(v.dtype)
        return jnp.einsum('bhgqk,bkhd->bqhgd', p, v)

    o = lax.map(one_block, qb)
    return o.transpose(1, 0, 2, 3, 4, 5).reshape(bsz, s_len, n_h * d)


def windowed_sink_gqa(q, k, v, sink, scale):
    bsz, s_len, n_h, d = q.shape
    n_kv = k.shape[2]
    grp = n_h // n_kv
    nb = s_len // BLOCK
    qb = q.reshape(bsz, nb, BLOCK, n_kv, grp, d)
    pad = ((0, 0), (BLOCK, BLOCK), (0, 0), (0, 0))
    kp = jnp.pad(k, pad).reshape(bsz, nb + 2, BLOCK, n_kv, d)
    vp = jnp.pad(v, pad).reshape(bsz, nb + 2, BLOCK, n_kv, d)
    kb = jnp.concatenate([kp[:, :-2], kp[:, 1:-1], kp[:, 2:]], axis=2)
    vb = jnp.concatenate([vp[:, :-2], vp[:, 1:-1], vp[:, 2:]], axis=2)
    s = jnp.einsum('bnqhgd,bnkhd->bnhgqk', qb, kb).astype(jnp.float32) * scale
    blk = jnp.arange(nb)[:, None]
    qpos = blk * BLOCK + jnp.arange(BLOCK)[None, :]
    kpos = (blk - 1) * BLOCK + jnp.arange(3 * BLOCK)[None, :]
    valid = ((jnp.abs(qpos[:, :, None] - kpos[:, None, :]) <= WINDOW)
             & (kpos[:, None, :] >= 0) & (kpos[:, None, :] < s_len))
    s = jnp.where(valid[None, :, None, None], s, -1e30)
    sk = sink.astype(jnp.float32).reshape(1, 1, n_kv, grp, 1, 1)
    m = jnp.maximum(jnp.max(s, axis=-1, keepdims=True), sk)
    p = jnp.exp(s - m)
    denom = jnp.sum(p, axis=-1, keepdims=True) + jnp.exp(sk - m)
    p = (p / denom).astype(v.dtype)
    o = jnp.einsum('bnhgqk,bnkhd->bnqhgd', p, vb)
    return o.reshape(bsz, s_len, n_h * d)


def mla_blocks(q_nope, q_rope, k_nope, k_rope, v, scale):
    bsz, s_len, n_h, dn = q_nope.shape
    dr = q_rope.shape[-1]
    dv = v.shape[-1]
    nb = s_len // BLOCK
    qn = q_nope.reshape(bsz, nb, BLOCK, n_h, dn).transpose(1, 0, 2, 3, 4)
    qr = q_rope.reshape(bsz, nb, BLOCK, n_h, dr).transpose(1, 0, 2, 3, 4)

    def one_block(args):
        qn_b, qr_b = args
        s = (jnp.einsum('bqhd,bkhd->bhqk', qn_b, k_nope)
             + jnp.einsum('bqhr,bkr->bhqk', qr_b, k_rope)).astype(jnp.float32) * scale
        p = jax.nn.softmax(s, axis=-1).astype(v.dtype)
        return jnp.einsum('bhqk,bkhd->bqhd', p, v)

    o = lax.map(one_block, (qn, qr))
    return o.transpose(1, 0, 2, 3, 4).reshape(bsz, s_len, n_h * dv)


def hybrid_layer(x, g_norm, w_in, a_qn, a_kn, b_sink, c_qn, c_kvn, c_wuq, c_wukv, w_out,
                 axial_row, axial_col, rope_1d, rope_mla):
    bsz, s_len, _ = x.shape
    h = rms_norm(x, g_norm)
    z = jnp.einsum('bsd,dc->bsc', h, w_in)
    idx = np.cumsum(IN_SPLITS)[:-1].tolist()
    aq, ak, av, ag, bq, bk, bv, bg, cq, ckv, ckr, cg = jnp.split(z, idx, axis=-1)

    qa = rms_norm(aq.reshape(bsz, s_len, A_HEADS, HEAD_DIM), a_qn)
    ka = rms_norm(ak.reshape(bsz, s_len, A_KV_HEADS, HEAD_DIM), a_kn)
    qa = apply_axial_rope(qa, axial_row, axial_col)
    ka = apply_axial_rope(ka, axial_row, axial_col)
    va = av.reshape(bsz, s_len, A_KV_HEADS, HEAD_DIM)
    oa = dense_gqa_blocks(qa, ka, va, HEAD_DIM ** -0.5) * jax.nn.silu(ag)

    qb = apply_rope(bq.reshape(bsz, s_len, B_HEADS, HEAD_DIM), rope_1d[0], rope_1d[1])
    kb = apply_rope(bk.reshape(bsz, s_len, B_KV_HEADS, HEAD_DIM), rope_1d[0], rope_1d[1])
    vb = bv.reshape(bsz, s_len, B_KV_HEADS, HEAD_DIM)
    ob = windowed_sink_gqa(qb, kb, vb, b_sink, HEAD_DIM ** -0.5) * jax.nn.silu(bg)

    qc = jnp.einsum('bsr,rc->bsc', rms_norm(cq, c_qn), c_wuq).reshape(bsz, s_len, C_HEADS, C_NOPE + C_ROPE)
    qc_nope = qc[..., :C_NOPE]
    qc_rope = apply_rope(qc[..., C_NOPE:], rope_mla[0], rope_mla[1])
    kv = jnp.einsum('bsr,rc->bsc', rms_norm(ckv, c_kvn), c_wukv).reshape(bsz, s_len, C_HEADS, C_NOPE + C_V)
    kc_nope = kv[..., :C_NOPE]
    vc = kv[..., C_NOPE:]
    kc_rope = apply_rope(ckr[:, :, None, :], rope_mla[0], rope_mla[1])[:, :, 0, :]
    oc = mla_blocks(qc_nope, qc_rope, kc_nope, kc_rope, vc, (C_NOPE + C_ROPE) ** -0.5) * jax.nn.silu(cg)

    o = jnp.concatenate([oa, ob, oc], axis=-1)
    return x + jnp.einsum('bsm,md->bsd', o, w_out)


def setup_inputs(seed: int = 0) -> dict:
    key = jax.random.key(seed)
    ks = jax.random.split(key, 13)
    f32 = jnp.float32

    def nrm(k, shape, scale):
        return jax.random.normal(k, shape, f32) * scale

    return {
        "x": nrm(ks[0], (BATCH, SEQ, D_MODEL), 1.0),
        "norm_g": 1.0 + nrm(ks[1], (DEPTH, D_MODEL), 0.02),
        "w_in": nrm(ks[2], (DEPTH, D_MODEL, IN_COLS), D_MODEL ** -0.5),
        "a_q_norm": 1.0 + nrm(ks[3], (DEPTH, HEAD_DIM), 0.02),
        "a_k_norm": 1.0 + nrm(ks[4], (DEPTH, HEAD_DIM), 0.02),
        "b_sink": nrm(ks[5], (DEPTH, B_HEADS), 0.5),
        "c_q_norm": 1.0 + nrm(ks[6], (DEPTH, C_Q_RANK), 0.02),
        "c_kv_norm": 1.0 + nrm(ks[7], (DEPTH, C_KV_RANK), 0.02),
        "c_w_uq": nrm(ks[8], (DEPTH, C_Q_RANK, C_HEADS * (C_NOPE + C_ROPE)), C_Q_RANK ** -0.5),
        "c_w_ukv": nrm(ks[9], (DEPTH, C_KV_RANK, C_HEADS * (C_NOPE + C_V)), C_KV_RANK ** -0.5),
        "w_out": nrm(ks[10], (DEPTH, MIX_W, D_MODEL), MIX_W ** -0.5),
        "final_g": 1.0 + nrm(ks[11], (D_MODEL,), 0.02),
    }


def reference(x, norm_g, w_in, a_q_norm, a_k_norm, b_sink, c_q_norm, c_kv_norm, c_w_uq, c_w_ukv, w_out, final_g):
    s_len = x.shape[1]
    rows = s_len // GRID_W
    t = jnp.arange(s_len)
    row_idx = jnp.repeat(jnp.arange(rows), GRID_W)
    col_idx = jnp.tile(jnp.arange(GRID_W), rows)
    axial_row = rope_tables(row_idx, HEAD_DIM // 2)
    axial_col = rope_tables(col_idx, HEAD_DIM // 2)
    rope_1d = rope_tables(t, HEAD_DIM)
    rope_mla = rope_tables(t, C_ROPE)

    h = x
    for l in range(DEPTH):
        h = hybrid_layer(h, norm_g[l], w_in[l], a_q_norm[l], a_k_norm[l], b_sink[l],
                         c_q_norm[l], c_kv_norm[l], c_w_uq[l], c_w_ukv[l], w_out[l],
                         axial_row, axial_col, rope_1d, rope_mla)
    return rms_norm(h, final_g)
```

```python
import numpy as np
from contextlib import ExitStack
import concourse.bass as bass
import concourse.mybir as mybir
from concourse.bass_utils import run_bass_kernel_spmd

F32 = mybir.dt.float32
BF16 = mybir.dt.bfloat16
ALU = mybir.AluOpType
AF = mybir.ActivationFunctionType
AX = mybir.AxisListType

S = 2048
D = 1024
NT = 16
DEPTH = 2
NSEQ = 2
EPS = 1e-6
INC = 2656

ENGS = ("pe", "act", "dve", "pool", "sp")


class _Op:
    __slots__ = ("eng", "fn", "deps", "signal", "ticket", "dma", "semkey", "ndma", "dma_val", "waits", "idx")


class Prog:
    def __init__(self, nc):
        self.nc = nc
        self.ops = []
        self.last_writer = {}
        self.readers = {}
        self.dma_counts = {}

    def _deps(self, reads, writes):
        deps = set()
        for c in reads:
            w = self.last_writer.get(c)
            if w is not None:
                deps.add(w)
        for c in writes:
            w = self.last_writer.get(c)
            if w is not None:
                deps.add(w)
            for r in self.readers.get(c, ()):
                deps.add(r)
        return deps

    def _commit(self, idx, reads, writes):
        for c in reads:
            self.readers.setdefault(c, []).append(idx)
        for c in writes:
            self.last_writer[c] = idx
            self.readers[c] = []

    def op(self, eng, fn, reads=(), writes=()):
        o = _Op()
        o.eng = eng; o.fn = fn; o.dma = False; o.signal = False; o.ticket = None
        o.idx = len(self.ops)
        reads = list(reads); writes = list(writes)
        o.deps = self._deps(reads, writes)
        self.ops.append(o)
        self._commit(o.idx, reads, writes)
        return o.idx

    def dma(self, fn, ndma, semkey, reads=(), writes=(), eng="sp"):
        o = _Op()
        o.eng = eng; o.fn = fn; o.dma = True; o.signal = False; o.ticket = None
        o.idx = len(self.ops)
        o.semkey = semkey; o.ndma = ndma
        self.dma_counts[semkey] = self.dma_counts.get(semkey, 0) + ndma
        o.dma_val = self.dma_counts[semkey] * 16
        reads = list(reads); writes = list(writes)
        o.deps = self._deps(reads, writes)
        self.ops.append(o)
        self._commit(o.idx, reads, writes)
        return o.idx

    def emit(self, final_wait_keys=()):
        nc = self.nc
        ops = self.ops
        for o in ops:
            for j in o.deps:
                d = ops[j]
                if d.dma:
                    continue
                if d.eng == o.eng and o.eng == "pe":
                    continue
                d.signal = True
        cnt = {e: 0 for e in ENGS}
        for o in ops:
            if o.signal and not o.dma:
                cnt[o.eng] += 1
                o.ticket = cnt[o.eng]
        with ExitStack() as es:
            sems = {e: es.enter_context(nc.semaphore("s_" + e)) for e in ("pe", "act", "dve", "pool")}
            dsems = {}
            for k in self.dma_counts:
                dsems[k] = es.enter_context(nc.semaphore("d_%d" % len(dsems)))
            waited = {e: {} for e in ENGS}
            for o in ops:
                w = {}
                for j in o.deps:
                    d = ops[j]
                    if d.dma:
                        key = ("d", d.semkey); val = d.dma_val
                    else:
                        if d.eng == o.eng and o.eng == "pe":
                            continue
                        key = ("e", d.eng); val = d.ticket
                    if w.get(key, 0) < val:
                        w[key] = val
                o.waits = []
                for key, val in w.items():
                    if waited[o.eng].get(key, 0) >= val:
                        continue
                    waited[o.eng][key] = val
                    sem = dsems[key[1]] if key[0] == "d" else sems[key[1]]
                    o.waits.append((sem, val))
            block = es.enter_context(nc.Block())

            def run(engname, eng):
                for o in ops:
                    if o.eng != engname:
                        continue
                    for sem, val in o.waits:
                        eng.wait_ge(sem, val)
                    if o.dma:
                        o.fn(eng, dsems[o.semkey])
                    else:
                        ins = o.fn(eng)
                        if o.signal:
                            ins.then_inc(sems[engname], 1)
                if engname == "sp":
                    for k in final_wait_keys:
                        eng.wait_ge(dsems[k], self.dma_counts[k] * 16)

            @block.tensor
            def _(eng):
                run("pe", eng)

            @block.scalar
            def _(eng):
                run("act", eng)

            @block.vector
            def _(eng):
                run("dve", eng)

            @block.gpsimd
            def _(eng):
                run("pool", eng)

            @block.sync
            def _(eng):
                run("sp", eng)


def build_program(depth=DEPTH, nseq=NSEQ):
    nc = bass.Bass("TRN2", target_bir_lowering=False)

    def din(name, shape):
        return nc.dram_tensor(name, list(shape), F32, kind="ExternalInput").ap()

    x_d = din("x", [nseq, S, D])
    win_d = din("w_in", [depth, D, INC])
    wout_d = din("w_out", [depth, D, D])
    wuq_d = din("c_w_uq", [depth, 192, 384])
    wukv_d = din("c_w_ukv", [depth, 128, 512])
    gx_d = din("gx", [depth, 128, D])
    gf_d = din("gf", [128, D])
    gqk_d = din("gqk", [depth, 128, 512])
    gcq_d = din("gcq", [depth, 128, 192])
    gckv_d = din("gckv", [depth, 128, 128])
    sink_d = din("sink", [depth, 128, 6])
    tabA_d = din("tabA", [NT, 128, 128])
    tabB_d = din("tabB", [NT, 128, 128])
    tabC_d = din("tabC", [NT, 128, 64])
    mask_d = din("maskb", [128, 1024])
    ident_d = din("ident", [128, 128])
    out_d = nc.dram_tensor("out", [nseq, S, D], F32, kind="ExternalOutput").ap()
    xs_d = out_d

    with ExitStack() as es:
        def sb(name, shape, dt=F32):
            return es.enter_context(nc.sbuf_tensor(name, list(shape), dt))

        def ps(name, shape, dt=F32):
            return es.enter_context(nc.psum_tensor(name, list(shape), dt))

        hT = sb("hT", [128, 8, S], BF16)
        oT = sb("oT", [128, 8, S], BF16)
        warena = [sb("warena%d" % i, [128, 8 * 1024], BF16) for i in range(2)]
        stages = [sb("stage%d" % i, [128, 1024]) for i in range(2)]
        stage = stages[0]
        qk = sb("qk", [128, 4, S], BF16)
        kc = sb("kc", [128, 4, S], BF16)
        cl = sb("cl", [128, 3, S], BF16)
        vext = sb("vext", [128, NT, 4, 128], BF16)
        tab = [sb("tab%d" % i, [128, 128]) for i in range(2)]
        xt = [sb("xt%d" % i, [128, D]) for i in range(2)]
        zs = [sb("zs%d" % i, [128, 512]) for i in range(2)]
        t1s = sb("t1s", [128, 512])
        t2s = sb("t2s", [128, 512])
        t3s = sb("t3s", [128, 512])
        zr = [sb("zr%d" % i, [128, 512], BF16) for i in range(2)]
        Pb = [sb("P%d" % i, [128, 1024], BF16) for i in range(3)]
        hbs = Pb[2]
        Rt = sb("Rt", [128, 512])
        tf = sb("tf", [128, 512])
        gx = sb("gx_t", [128, D])
        gqk = sb("gqk_t", [128, 512])
        gcq = sb("gcq_t", [128, 192])
        gckv = sb("gckv_t", [128, 128])
        sinkt = sb("sink_t", [128, 6])
        es_t = sb("es_t", [128, 6])
        maskb = sb("maskb_t", [128, 4, 256], BF16)
        ident = sb("ident_t", [128, 128], BF16)
        wuq0 = sb("wuq0", [128, 384], BF16)
        wuq1 = sb("wuq1", [128, 384], BF16)
        wukv = sb("wukv", [128, 512], BF16)
        sm = [sb("sm%d" % i, [128, 16]) for i in range(2)]
        sm2 = [sb("sm2_%d" % i, [128, 16]) for i in range(2)]
        epst = sb("epst", [128, 1])
        krp = [sb("krp%d" % i, [128, 128], BF16) for i in range(2)]

        pz = ps("pz", [128, 1024])
        ptr = pz[:, 512:1024].bitcast(BF16)
        Sp = [ps("S%d" % i, [128, 1024]) for i in range(2)]
        Op2 = ps("Op", [128, 1024])
        Op = Op2[:, 0:512]

        P = Prog(nc)

        def DMA(out, in_, semkey, reads=(), writes=()):
            P.dma(lambda e, s: e.dma_start(out=out, in_=in_).then_inc(s, 16), 1, semkey, reads=reads, writes=writes)

        def seq(units):
            for u in units:
                u()

        def inter(main, side):
            nm, ns = len(main), len(side)
            si = 0
            for i, u in enumerate(main):
                u()
                target = ((i + 1) * ns) // nm
                while si < target:
                    side[si]()
                    si += 1
            while si < ns:
                side[si]()
                si += 1

        def skew(tiles):
            nt = len(tiles)
            ns = max(len(t) for t in tiles)
            units = []
            for step in range(nt + ns - 1):
                fs = []
                for st in range(ns):
                    t = step - st
                    if 0 <= t < nt and st < len(tiles[t]):
                        fs.append(tiles[t][st])
                units.append(lambda fs=fs: [f() for f in fs])
            return units

        DMA(stage[:, 0:128], ident_d, ("stage", 0), writes=[("stage", 0)])
        P.op("pool", lambda e: e.tensor_copy(out=ident[:], in_=stage[:, 0:128]), reads=[("stage", 0)], writes=["ident"])
        DMA(stage[:, 0:1024], mask_d, ("stage", 0), writes=[("stage", 0)])
        P.op("pool", lambda e: e.tensor_copy(out=maskb[:].rearrange("p r a -> p (r a)"), in_=stage[:, 0:1024]),
             reads=[("stage", 0)], writes=["maskb"])
        P.op("pool", lambda e: e.memset(epst[:], EPS), writes=["epst"])
        for i in range(2):
            P.op("pool", lambda e, i=i: e.memset(krp[i][:], 0.0), writes=[("krp", i)])
        for h in range(4):
            lo = 64 if h % 2 == 0 else 0
            P.op("pool", lambda e, h=h, lo=lo: e.memset(vext[:, :, h, lo:lo + 64], 1.0), writes=[("vones", h)])

        def rsqrt_small(col0, n, scale, p, extra_reads=()):
            src = sm[p][:, col0:col0 + n]
            dst = sm2[p][:, col0:col0 + n]
            P.op("act", lambda e: e.activation(out=dst, in_=src, func=AF.Ln, bias=epst[:], scale=scale),
                 reads=[("sm", p), "epst"] + list(extra_reads), writes=[("sm2", p, col0)])
            P.op("act", lambda e: e.activation(out=dst, in_=dst, func=AF.Exp, scale=-0.5),
                 reads=[("sm2", p, col0)], writes=[("sm2", p, col0)])

        def rope(eng, x, H, G, Dh, tcos, tsin, out_a, out_b, rcells, p):
            R = G * 2 * Dh
            x3 = x.rearrange("p (h r) -> p h r", h=H)
            P.op(eng, lambda e: e.tensor_tensor(out=out_a.rearrange("p (h r) -> p h r", h=H), in0=x3,
                                                in1=tcos.unsqueeze(1).to_broadcast([128, H, R]), op=ALU.mult),
                 reads=list(rcells) + [("tab", p)], writes=[("t2",)])
            x5 = x.rearrange("p (h g t d) -> p h g t d", h=H, g=G, t=2, d=Dh)
            b5 = out_b.rearrange("p (h g t d) -> p h g t d", h=H, g=G, t=2, d=Dh)
            s4 = tsin.rearrange("p (g t d) -> p g t d", g=G, t=2, d=Dh)
            for t in range(2):
                P.op(eng, lambda e, t=t: e.tensor_tensor(
                    out=b5[:, :, :, t, :], in0=x5[:, :, :, 1 - t, :],
                    in1=s4[:, :, t, :].unsqueeze(1).to_broadcast([128, H, G, Dh]), op=ALU.mult),
                    reads=list(rcells) + [("tab", p)], writes=[("t3", t)])

        WAs = [w[:, 0:5120].rearrange("p (c n) -> p c n", c=8) for w in warena]
        WGs = [w[:, 5120:8192].rearrange("p (c n) -> p c n", c=8) for w in warena]
        WOs = [w[:].rearrange("p (j n) -> p j n", j=8) for w in warena]

        def Wc(a):
            return [("W", a, c) for c in range(8)]

        def hT_cells(tts):
            return [("hT", t) for t in tts]

        def units_load_win(l, col0, ncols, casts, a):
            units = []
            for c in range(8):
                def u(c=c):
                    st = stages[c % 2]
                    sk = ("stage", c % 2)
                    DMA(st[:, 0:ncols], win_d[l, c * 128:(c + 1) * 128, col0:col0 + ncols], sk, writes=[sk])
                    for fn in casts:
                        P.op("pool", lambda e, c=c, fn=fn, st=st: fn(e, c, st), reads=[sk], writes=[("W", a, c)])
                units.append(u)
            return units

        def gate_tiles(ncol_groups, slots, a):
            tiles = []
            k = 0
            for gi in range(ncol_groups):
                for tc in range(4):
                    tiles.append(_gate_tile(gi, tc, slots[gi], a, k % 2))
                    k += 1
            return tiles

        def oTc(ot, ohs, col0, ncols):
            return [("oT", ot, oh, q) for oh in ohs for q in range(col0 // 256, (col0 + ncols) // 256)]

        def _gate_tile(gi, tc, slot, a, bk):
            csl = slice(tc * 512, (tc + 1) * 512)

            pzb = pz[:, bk * 512:(bk + 1) * 512]
            pcell = "pz" if bk == 0 else "pz2"

            def g1():
                def f(e):
                    ins = None
                    for c in range(8):
                        ins = e.matmul(pzb, lhsT=WGs[a][:, c, gi * 128:(gi + 1) * 128], rhs=hT[:, c, csl],
                                       start=(c == 0), stop=(c == 7))
                    return ins
                P.op("pe", f, reads=Wc(a) + hT_cells(range(4 * tc, 4 * tc + 4)), writes=[pcell])
                P.op("act", lambda e: e.activation(out=oT[:, slot, csl], in_=pzb, func=AF.Silu),
                     reads=[pcell], writes=oTc(slot, (0, 1), tc * 512, 512))
            return [g1]

        def units_attention(heads):
            steps = []
            for hd in heads:
                CQ = 256 if hd["window"] else 512
                for qc in range(S // CQ):
                    if hd["window"]:
                        kts = [kt for kt in range(2 * qc - 1, 2 * qc + 3) if 0 <= kt < NT]
                        groups = [kts]
                    else:
                        kts = list(range(NT))
                        groups = [kts[i:i + 2] for i in range(0, NT, 2)]
                    for gi, g in enumerate(groups):
                        steps.append(dict(hd=hd, qc=qc, CQ=CQ, kts=g, first=(gi == 0), last=(gi == len(groups) - 1)))

            def emit_qk(i, st):
                hd = st["hd"]; CQ = st["CQ"]; qc = st["qc"]; sbuf = Sp[i % 2]
                win = hd["window"]
                r0 = (st["kts"][0] - (2 * qc - 1)) if win else 0

                def f(e):
                    ins = None
                    for n, kt in enumerate(st["kts"]):
                        ins = e.matmul(sbuf[:, n * CQ:(n + 1) * CQ], lhsT=hd["k"][:, kt * 128:(kt + 1) * 128],
                                       rhs=hd["q"][:, qc * CQ:(qc + 1) * CQ], start=True, stop=(not win))
                        if win:
                            ins = e.matmul(sbuf[:, n * CQ:(n + 1) * CQ], lhsT=ident[:], rhs=maskb[:, r0 + n, :],
                                           start=False, stop=True)
                    return ins
                P.op("pe", f, reads=hd["rq"] + hd["rk"] + (["maskb", "ident"] if win else []), writes=[("S", i % 2)])

            def emit_exp(i, st):
                hd = st["hd"]; CQ = st["CQ"]; n = len(st["kts"]); sbuf = Sp[i % 2]; pb = Pb[i % 3]
                P.op("act", lambda e: e.activation(out=pb[:, 0:n * CQ], in_=sbuf[:, 0:n * CQ], func=AF.Exp, scale=hd["scale"]),
                     reads=[("S", i % 2)], writes=[("P", i % 3)])

            def emit_pv(i, st):
                hd = st["hd"]; CQ = st["CQ"]; qc = st["qc"]; pb = Pb[i % 3]
                nk = len(st["kts"])

                def f(e):
                    ins = None
                    for n, kt in enumerate(st["kts"]):
                        ins = e.matmul(Op[:, 0:CQ], lhsT=vext[:, kt, hd["vh"], :], rhs=pb[:, n * CQ:(n + 1) * CQ],
                                       start=(st["first"] and n == 0), stop=(st["last"] and n == nk - 1))
                    return ins
                P.op("pe", f, reads=[("P", i % 3), ("vext", hd["vh"]), ("vones", hd["vh"])], writes=["Op"])
                if st["last"]:
                    oh = hd["ohalf"]
                    po = slice(64 * oh, 64 * oh + 64)
                    psl = slice(64 * (1 - oh), 64 * (1 - oh) + 64)
                    co = (qc % 2) * 256 if CQ == 256 else 0
                    tcs = [("tf", oh, k) for k in range(co // 256, (co + CQ) // 256)]
                    rcs = [("Rt", oh, k) for k in range(co // 256, (co + CQ) // 256)]
                    if hd["sink"] is not None:
                        sc = hd["sink"]
                        lnf = lambda e: e.activation(out=Rt[po, co:co + CQ], in_=Op[psl, 0:CQ], func=AF.Ln, bias=es_t[po, sc:sc + 1])
                        lnr = ["Op", "es"]
                    else:
                        lnf = lambda e: e.activation(out=Rt[po, co:co + CQ], in_=Op[psl, 0:CQ], func=AF.Ln)
                        lnr = ["Op"]
                    if hd.get("evac", "act") == "act":
                        P.op("act", lambda e: e.activation(out=tf[po, co:co + CQ], in_=Op[po, 0:CQ], func=AF.Copy), reads=["Op"], writes=tcs)
                    else:
                        P.op("dve", lambda e: e.tensor_copy(out=tf[po, co:co + CQ], in_=Op[po, 0:CQ]), reads=["Op"], writes=tcs + ["oplock"])
                        lnr = lnr + ["oplock"]
                    P.op("act", lnf, reads=lnr, writes=rcs)

            def emit_fin2(st):
                hd = st["hd"]; CQ = st["CQ"]; qc = st["qc"]
                oh = hd["ohalf"]
                po = slice(64 * oh, 64 * oh + 64)
                co = (qc % 2) * 256 if CQ == 256 else 0
                tcs = [("tf", oh, k) for k in range(co // 256, (co + CQ) // 256)]
                rcs = [("Rt", oh, k) for k in range(co // 256, (co + CQ) // 256)]
                R = Rt[po, co:co + CQ]
                T = tf[po, co:co + CQ]
                P.op("act", lambda e: e.activation(out=R, in_=R, func=AF.Exp, scale=-1.0), reads=rcs, writes=rcs)
                P.op("dve", lambda e: e.tensor_tensor(out=T, in0=T, in1=R, op=ALU.mult), reads=tcs + rcs, writes=tcs)
                cs = slice(qc * CQ, (qc + 1) * CQ)
                occ = oTc(hd["ot"], (oh,), qc * CQ, CQ)
                P.op("pool", lambda e: e.tensor_tensor(out=oT[po, hd["ot"], cs], in0=T, in1=oT[po, hd["ot"], cs], op=ALU.mult),
                     reads=tcs + occ, writes=occ)

            units = []
            n = len(steps)
            for i in range(n + 3):
                def u(i=i):
                    if i < n:
                        emit_qk(i, steps[i])
                        emit_exp(i, steps[i])
                    if 3 <= i and steps[i - 3]["last"]:
                        emit_fin2(steps[i - 3])
                    if 2 <= i <= n + 1:
                        emit_pv(i - 2, steps[i - 2])
                units.append(u)
            return units

        def units_attention_pairs(pairs):
            steps = []
            for pr in pairs:
                CQ = 256 if pr["window"] else 512
                for qc in range(S // CQ):
                    if pr["window"]:
                        kts = [kt for kt in range(2 * qc - 1, 2 * qc + 3) if 0 <= kt < NT]
                        groups = [kts[0:2], kts[2:4]]
                    else:
                        kts = list(range(NT))
                        groups = [[kt] for kt in kts]
                    groups = [g for g in groups if g]
                    for gi, g in enumerate(groups):
                        steps.append(dict(pr=pr, qc=qc, CQ=CQ, kts=g, kfirst=kts[0], klast=kts[-1],
                                          last=(gi == len(groups) - 1),
                                          blocks=[(kt, hf) for kt in g for hf in (0, 1)]))

            def col(st, n, hf):
                return hf * 512 + n * st["CQ"]

            def emit_qk(i, st):
                pr = st["pr"]; CQ = st["CQ"]; qc = st["qc"]; sbuf = Sp[i % 2]
                win = pr["window"]

                def f(e):
                    ins = None
                    for n, kt in enumerate(st["kts"]):
                        for hf in (0, 1):
                            prt = slice(64 * hf, 64 * hf + 64)
                            c0 = col(st, n, hf)
                            ins = e.matmul(sbuf[:, c0:c0 + CQ], lhsT=pr["k"][prt, kt * 128:(kt + 1) * 128],
                                           rhs=pr["q"][prt, qc * CQ:(qc + 1) * CQ], start=True, stop=(not win))
                        if win:
                            r = kt - (2 * qc - 1)
                            for hf in (0, 1):
                                c0 = col(st, n, hf)
                                ins = e.matmul(sbuf[:, c0:c0 + CQ], lhsT=ident[:], rhs=maskb[:, r, :], start=False, stop=True)
                    return ins
                P.op("pe", f, reads=pr["rq"] + pr["rk"] + (["maskb", "ident"] if win else []), writes=[("S", i % 2)])

            def emit_exp(i, st):
                pr = st["pr"]; CQ = st["CQ"]; nk = len(st["kts"]); sbuf = Sp[i % 2]; pb = Pb[i % 3]
                if nk * CQ == 512:
                    P.op("act", lambda e: e.activation(out=pb[:, 0:1024], in_=sbuf[:, 0:1024], func=AF.Exp, scale=pr["scale"]),
                         reads=[("S", i % 2)], writes=[("P", i % 3)])
                else:
                    w = nk * CQ
                    for hf in (0, 1):
                        P.op("act", lambda e, hf=hf: e.activation(out=pb[:, hf * 512:hf * 512 + w], in_=sbuf[:, hf * 512:hf * 512 + w],
                                                                   func=AF.Exp, scale=pr["scale"]),
                             reads=[("S", i % 2)], writes=[("P", i % 3)])

            def emit_pv(i, st):
                pr = st["pr"]; CQ = st["CQ"]; qc = st["qc"]; pb = Pb[i % 3]
                vb = pr["vbase"]

                def f(e):
                    ins = None
                    for n, kt in enumerate(st["kts"]):
                        for hf in (0, 1):
                            c0 = col(st, n, hf)
                            ins = e.matmul(Op2[:, hf * 512:hf * 512 + CQ], lhsT=vext[:, kt, vb + hf, :], rhs=pb[:, c0:c0 + CQ],
                                           start=(kt == st["kfirst"]), stop=(kt == st["klast"]))
                    return ins
                P.op("pe", f, reads=[("P", i % 3), ("vext", vb), ("vext", vb + 1), ("vones", vb), ("vones", vb + 1)], writes=["Op"])
                if st["last"]:
                    co = (qc % 2) * 256 if CQ == 256 else 0
                    kr = range(co // 256, (co + CQ) // 256)
                    for hf in (0, 1):
                        po = slice(64 * hf, 64 * hf + 64)
                        psl = slice(64 * (1 - hf), 64 * (1 - hf) + 64)
                        acc = Op2[:, hf * 512:hf * 512 + CQ]
                        tcs = [("tf", hf, k) for k in kr]
                        rcs = [("Rt", hf, k) for k in kr]
                        P.op("act", lambda e, po=po, acc=acc: e.activation(out=tf[po, co:co + CQ], in_=acc[po, :], func=AF.Copy),
                             reads=["Op"], writes=tcs)
                        if pr["sinks"] is not None:
                            sc = pr["sinks"][hf]
                            P.op("act", lambda e, po=po, psl=psl, acc=acc, sc=sc: e.activation(
                                out=Rt[po, co:co + CQ], in_=acc[psl, :], func=AF.Ln, bias=es_t[po, sc:sc + 1]),
                                reads=["Op", "es"], writes=rcs)
                        else:
                            P.op("act", lambda e, po=po, psl=psl, acc=acc: e.activation(out=Rt[po, co:co + CQ], in_=acc[psl, :], func=AF.Ln),
                                 reads=["Op"], writes=rcs)

            def emit_fin2(st):
                pr = st["pr"]; CQ = st["CQ"]; qc = st["qc"]
                co = (qc % 2) * 256 if CQ == 256 else 0
                kr = range(co // 256, (co + CQ) // 256)
                tcs = [("tf", hf, k) for hf in (0, 1) for k in kr]
                rcs = [("Rt", hf, k) for hf in (0, 1) for k in kr]
                R = Rt[:, co:co + CQ]
                T = tf[:, co:co + CQ]
                P.op("act", lambda e: e.activation(out=R, in_=R, func=AF.Exp, scale=-1.0), reads=rcs, writes=rcs)
                P.op("dve", lambda e: e.tensor_tensor(out=T, in0=T, in1=R, op=ALU.mult), reads=tcs + rcs, writes=tcs)
                cs = slice(qc * CQ, (qc + 1) * CQ)
                occ = oTc(pr["ot"], (0, 1), qc * CQ, CQ)
                P.op("pool", lambda e: e.tensor_tensor(out=oT[:, pr["ot"], cs], in0=T, in1=oT[:, pr["ot"], cs], op=ALU.mult),
                     reads=tcs + occ, writes=occ)

            units = []
            n = len(steps)
            for i in range(n + 3):
                def u(i=i):
                    if i < n:
                        emit_qk(i, steps[i])
                        emit_exp(i, steps[i])
                    if 3 <= i and steps[i - 3]["last"]:
                        emit_fin2(steps[i - 3])
                    if 2 <= i <= n + 1:
                        emit_pv(i - 2, steps[i - 2])
                units.append(u)
            return units

        def pairs_AB(mix):
            isA = (mix == 0)
            srcq = qk if isA else kc
            dname = "qk" if isA else "kc"
            prs = []
            for j in range(3):
                prs.append(dict(q=srcq[:, j, :], k=srcq[:, 3, :], vbase=(0 if isA else 2), scale=0.125, window=(not isA),
                                sinks=(None if isA else (j, j + 3)), ot=3 * mix + j,
                                rq=[(dname, j, t) for t in range(NT)], rk=[(dname, 3, t) for t in range(NT)]))
            return prs

        def oT_cells_all():
            cells = []
            for j in range(8):
                cells += oTc(j, (0, 1), 0, S)
            return cells

        def transposes(src_fn, n, reads, dst_ap, dst_cells, rows=128):
            def f(e):
                ins = None
                for i in range(n):
                    ins = e.transpose(out=ptr[0:rows, i * 128:(i + 1) * 128], in_=src_fn(i), identity=ident[:])
                return ins
            P.op("pe", f, reads=list(reads) + ["ident"], writes=["pz2"])
            P.op("dve", lambda e: e.tensor_copy(out=dst_ap, in_=ptr[0:rows, 0:n * 128].rearrange("p (i t) -> p i t", i=n)),
                 reads=["pz2"], writes=dst_cells)

        xring = [(xt[0], ("xt", 0)), (xt[1], ("xt", 1)), (stages[0], ("stage", 0)), (stages[1], ("stage", 1))]

        def units_N(s, l, src):
            tiles = []
            for tt in range(NT):
                tiles.append(_n_tile(s, l, src, tt, tt % 2))
            return skew(tiles)

        def _n_tile(s, l, src, tt, p):
            xb, xc = xring[tt % 2]

            def s1():
                DMA(xb[:], src[s, tt * 128:(tt + 1) * 128, :], xc, reads=[("xs", s, tt)], writes=[xc])
                P.op("pool", lambda e: e.memset(sm[p][:, 0:1], 0.0), writes=[("sm", p)])
                P.op("act", lambda e: e.activation(out=Sp[1][:], in_=xb[:], func=AF.Square, accum_out=sm[p][:, 0:1]),
                     reads=[xc, ("sm", p)], writes=[("S", 1), ("sm", p)])
                rsqrt_small(0, 1, 1.0 / D, p)

            def s2():
                P.op("dve", lambda e: e.scalar_tensor_tensor(out=hbs[:], in0=xb[:], scalar=sm2[p][:, 0:1], in1=gx[:],
                                                             op0=ALU.mult, op1=ALU.mult),
                     reads=[xc, ("sm2", p, 0), "gx"], writes=[("P", 2)])
                transposes(lambda i: hbs[:, i * 128:(i + 1) * 128], 8, [("P", 2)],
                           hT[:, :, tt * 128:(tt + 1) * 128], [("hT", tt)])
            return [s1, s2]

        def units_projAB(s, l, mix):
            isA = (mix == 0)
            c0 = 1024 * mix
            dstq = qk if isA else kc
            dname = "qk" if isA else "kc"
            vbase = 0 if isA else 2

            a = mix
            WA = WAs[a]
            WG = WGs[a]

            def cast_q(e, c, st):
                return e.tensor_copy(out=WA[:, c, 0:384].rearrange("p (j hf d) -> p j hf d", j=3, hf=2),
                                     in_=st[:, 0:384].rearrange("p (hf j d) -> p j hf d", hf=2, j=3))

            def cast_kv(e, c, st):
                return e.tensor_copy(out=WA[:, c, 384:640], in_=st[:, 384:640])

            def cast_g(e, c, st):
                return e.tensor_copy(out=WG[:, c, :].rearrange("p (j hf d) -> p j hf d", j=3, hf=2),
                                     in_=st[:, 640:1024].rearrange("p (hf j d) -> p j hf d", hf=2, j=3))
            lunits = units_load_win(l, c0, 1024, [cast_q, cast_kv, cast_g], a)
            tabd = tabA_d if isA else tabB_d
            tiles = []
            for tt in range(NT):
                tiles.append(_ab_tile(isA, tabd, dstq, dname, vbase, tt, tt % 2, a))
            return lunits, skew(tiles), skew(gate_tiles(3, [3 * mix + j for j in range(3)], a))

        def _ab_tile(isA, tabd, dstq, dname, vbase, tt, p, a):
            tsl = slice(tt * 128, (tt + 1) * 128)
            WA = WAs[a]
            if (not isA) and p == 1:
                acc = Sp[1]
                c_qk = [("S", 1)]
                c_v = [("S", 1)]
            else:
                acc = pz
                c_qk = ["pz"]
                c_v = ["pz2"]

            def s1():
                def f(e):
                    ins = None
                    for c in range(8):
                        ins = e.matmul(acc[:, 0:512], lhsT=hT[:, c, tsl], rhs=WA[:, c, 0:512], start=(c == 0), stop=(c == 7))
                    for c in range(8):
                        ins = e.matmul(acc[:, 512:640], lhsT=hT[:, c, tsl], rhs=WA[:, c, 512:640], start=(c == 0), stop=(c == 7))
                    return ins
                P.op("pe", f, reads=Wc(a) + [("hT", tt)], writes=list(set(c_qk + c_v)))
                DMA(tab[p][:], tabd[tt], ("tab", p), writes=[("tab", p)])
                P.op("dve", lambda e: e.tensor_copy(out=zs[p][:, 0:512], in_=acc[:, 0:512]), reads=c_qk, writes=[("zs", p)])
                for kv in range(2):
                    lo = 0 if kv == 0 else 64
                    P.op("dve", lambda e, kv=kv, lo=lo: e.tensor_copy(out=vext[:, tt, vbase + kv, lo:lo + 64],
                                                                     in_=acc[:, 512 + kv * 64:576 + kv * 64]),
                         reads=c_v, writes=[("vext", vbase + kv)])
                if isA:
                    P.op("pool", lambda e: e.tensor_tensor(out=t1s[:], in0=zs[p][:, 0:512], in1=zs[p][:, 0:512], op=ALU.mult),
                         reads=[("zs", p)], writes=[("t1",)])
                    P.op("dve", lambda e: e.tensor_reduce(out=sm[p][:, 0:8], in_=t1s[:].rearrange("p (h d) -> p h d", d=64),
                                                          axis=AX.X, op=ALU.add), reads=[("t1",)], writes=[("sm", p)])

            def s2():
                if isA:
                    rsqrt_small(0, 8, 1.0 / 64, p)
                    P.op("dve", lambda e: e.tensor_tensor(out=t1s[:].rearrange("p (h d) -> p h d", d=64),
                                                          in0=zs[p][:, 0:512].rearrange("p (h d) -> p h d", d=64),
                                                          in1=sm2[p][:, 0:8].unsqueeze(2).to_broadcast([128, 8, 64]), op=ALU.mult),
                         reads=[("zs", p), ("sm2", p, 0)], writes=[("t1",)])
                    P.op("dve", lambda e: e.tensor_tensor(out=t1s[:], in0=t1s[:], in1=gqk[:], op=ALU.mult),
                         reads=[("t1",), "gqk"], writes=[("t1",)])
                    rope("pool", t1s[:], 8, 2, 16, tab[p][:, 0:64], tab[p][:, 64:128], t2s[:], t3s[:], [("t1",)], p)
                else:
                    rope("pool", zs[p][:, 0:512], 8, 1, 32, tab[p][:, 0:64], tab[p][:, 64:128], t2s[:], t3s[:], [("zs", p)], p)
                P.op("dve", lambda e: e.tensor_tensor(out=zr[p][:], in0=t2s[:], in1=t3s[:], op=ALU.add),
                     reads=[("t2",), ("t3", 0), ("t3", 1)], writes=[("zr", p)])

            def s3():
                transposes(lambda i: zr[p][:, i * 128:(i + 1) * 128], 4, [("zr", p)],
                           dstq[:, :, tsl], [(dname, j, tt) for j in range(4)])
            return [s1, s2, s3]

        def heads_AB(mix):
            isA = (mix == 0)
            srcq = qk if isA else kc
            dname = "qk" if isA else "kc"
            vbase = 0 if isA else 2
            heads = []
            for j in range(3):
                for hf in range(2):
                    h = j + 3 * hf
                    prt = slice(64 * hf, 64 * hf + 64)
                    heads.append(dict(q=srcq[prt, j, :], k=srcq[prt, 3, :], vh=vbase + hf, ohalf=hf, scale=0.125,
                                      window=(not isA), sink=(None if isA else h), ot=3 * mix + j, gslot=j,
                                      rq=[(dname, j, t) for t in range(NT)], rk=[(dname, 3, t) for t in range(NT)]))
            return heads

        def units_projC1(s, l):
            WA = WAs[0]
            WG = WGs[0]

            def cast_c(e, c, st):
                return e.tensor_copy(out=WA[:, c, 0:352], in_=st[:, 0:352])

            def cast_cg(e, c, st):
                return e.tensor_copy(out=WG[:, c, 0:256], in_=st[:, 352:608])
            units = units_load_win(l, 2048, 608, [cast_c, cast_cg], 0)

            def uw():
                sk = ("stage", 0)
                DMA(stage[:, 0:384], wuq_d[l, 0:128, :], sk, writes=[sk])
                P.op("pool", lambda e: e.tensor_copy(out=wuq0[:], in_=stage[:, 0:384]), reads=[sk], writes=["wuq0"])
                DMA(stage[0:64, 0:384], wuq_d[l, 128:192, :], sk, writes=[sk])
                P.op("pool", lambda e: e.tensor_copy(out=wuq1[0:64, :], in_=stage[0:64, 0:384]), reads=[sk], writes=["wuq1"])
                DMA(stage[:, 0:512], wukv_d[l], sk, writes=[sk])
                P.op("pool", lambda e: e.tensor_copy(out=wukv[:], in_=stage[:, 0:512]), reads=[sk], writes=["wukv"])
            units.append(uw)
            tiles = [_c_tile(tt, tt % 2) for tt in range(NT)]
            units += skew([t[0:3] for t in tiles])
            return units, skew([t[3:6] for t in tiles])

        def _c_tile(tt, p):
            tsl = slice(tt * 128, (tt + 1) * 128)
            z = zs[p]
            WA = WAs[0]

            def l1():
                def f(e):
                    ins = None
                    for c in range(8):
                        ins = e.matmul(pz[:, 0:352], lhsT=hT[:, c, tsl], rhs=WA[:, c, 0:352], start=(c == 0), stop=(c == 7))
                    return ins
                P.op("pe", f, reads=Wc(0) + [("hT", tt)], writes=["pz"])
                DMA(tab[p][:, 0:64], tabC_d[tt], ("tab", p), writes=[("tab", p)])
                P.op("dve", lambda e: e.tensor_copy(out=z[:, 0:352], in_=pz[:, 0:352]), reads=["pz"], writes=[("zs", p)])
                P.op("pool", lambda e: e.tensor_tensor(out=t1s[:, 0:320], in0=z[:, 0:320], in1=z[:, 0:320], op=ALU.mult),
                     reads=[("zs", p)], writes=[("t1",)])
                P.op("dve", lambda e: e.tensor_reduce(out=sm[p][:, 0:1], in_=t1s[:, 0:192], axis=AX.X, op=ALU.add),
                     reads=[("t1",)], writes=[("sm", p)])
                P.op("dve", lambda e: e.tensor_reduce(out=sm[p][:, 1:2], in_=t1s[:, 192:320], axis=AX.X, op=ALU.add),
                     reads=[("t1",), ("sm", p)], writes=[("sm", p)])

            def l2():
                rsqrt_small(0, 1, 1.0 / 192, p)
                rsqrt_small(1, 1, 1.0 / 128, p)
                P.op("dve", lambda e: e.scalar_tensor_tensor(out=zr[p][:, 0:192], in0=z[:, 0:192], scalar=sm2[p][:, 0:1], in1=gcq[:],
                                                             op0=ALU.mult, op1=ALU.mult),
                     reads=[("zs", p), ("sm2", p, 0), "gcq"], writes=[("zr", p)])
                P.op("dve", lambda e: e.scalar_tensor_tensor(out=zr[p][:, 192:320], in0=z[:, 192:320], scalar=sm2[p][:, 1:2], in1=gckv[:],
                                                             op0=ALU.mult, op1=ALU.mult),
                     reads=[("zs", p), ("sm2", p, 1), "gckv", ("zr", p)], writes=[("zr", p)])
                rope("pool", z[:, 320:352], 1, 1, 16, tab[p][:, 0:32], tab[p][:, 32:64], t2s[:, 0:32], t3s[:, 0:32], [("zs", p)], p)
                P.op("dve", lambda e: e.tensor_tensor(out=krp[p][:, 64:96], in0=t2s[:, 0:32], in1=t3s[:, 0:32], op=ALU.add),
                     reads=[("t2",), ("t3", 0), ("t3", 1), ("krp", p)], writes=[("krp", p)])

            def l3():
                def ftr(e):
                    e.transpose(out=ptr[:, 0:128], in_=zr[p][:, 0:128], identity=ident[:])
                    e.transpose(out=ptr[0:64, 128:256], in_=zr[p][:, 128:192], identity=ident[:])
                    e.transpose(out=ptr[:, 256:384], in_=zr[p][:, 192:320], identity=ident[:])
                    return e.transpose(out=ptr[0:96, 384:512], in_=krp[p][:, 0:96], identity=ident[:])
                P.op("pe", ftr, reads=[("zr", p), ("krp", p), "ident"], writes=["pz2"])
                P.op("dve", lambda e: e.tensor_copy(out=cl[:, 0, tsl], in_=ptr[:, 0:128]), reads=["pz2"], writes=[("cl", 0, tt)])
                P.op("dve", lambda e: e.tensor_copy(out=cl[0:64, 1, tsl], in_=ptr[0:64, 128:256]), reads=["pz2"], writes=[("cl", 1, tt)])
                P.op("dve", lambda e: e.tensor_copy(out=cl[:, 2, tsl], in_=ptr[:, 256:384]), reads=["pz2"], writes=[("cl", 2, tt)])
                P.op("dve", lambda e: e.tensor_copy(out=cl[64:96, 1, tsl], in_=ptr[64:96, 384:512]), reads=["pz2"], writes=[("cl", 3, tt)])

            def q1():
                def fq(e):
                    e.matmul(pz[:, 512:896], lhsT=cl[:, 0, tsl], rhs=wuq0[:], start=True, stop=False)
                    return e.matmul(pz[:, 512:896], lhsT=cl[0:64, 1, tsl], rhs=wuq1[0:64, :], start=False, stop=True)
                P.op("pe", fq, reads=[("cl", 0, tt), ("cl", 1, tt), "wuq0", "wuq1"], writes=["pz2"])
                DMA(tab[p][:, 0:64], tabC_d[tt], ("tab", p), writes=[("tab", p)])
                P.op("dve", lambda e: e.tensor_copy(out=z[:, 0:384], in_=pz[:, 512:896]), reads=["pz2"], writes=[("zs", p)])

            def q2():
                z3 = z[:, 0:384].rearrange("p (h r) -> p h r", h=4)
                zr3 = zr[p][:, 0:384].rearrange("p (h r) -> p h r", h=4)
                t2v = t2s[:, 0:128].rearrange("p (h r) -> p h r", h=4)
                t3v = t3s[:, 0:128].rearrange("p (h r) -> p h r", h=4)
                P.op("pool", lambda e: e.tensor_copy(out=zr3[:, :, 0:64], in_=z3[:, :, 0:64]), reads=[("zs", p)], writes=[("zr", p)])
                P.op("pool", lambda e: e.tensor_tensor(out=t2v, in0=z3[:, :, 64:96],
                                                       in1=tab[p][:, 0:32].unsqueeze(1).to_broadcast([128, 4, 32]), op=ALU.mult),
                     reads=[("zs", p), ("tab", p)], writes=[("t2",)])
                for t in range(2):
                    P.op("pool", lambda e, t=t: e.tensor_tensor(
                        out=t3v[:, :, 16 * t:16 * t + 16], in0=z3[:, :, 64 + 16 * (1 - t):80 + 16 * (1 - t)],
                        in1=tab[p][:, 32 + 16 * t:48 + 16 * t].unsqueeze(1).to_broadcast([128, 4, 16]), op=ALU.mult),
                        reads=[("zs", p), ("tab", p)], writes=[("t3", t)])
                P.op("dve", lambda e: e.tensor_tensor(out=zr3[:, :, 64:96], in0=t2v, in1=t3v, op=ALU.add),
                     reads=[("t2",), ("t3", 0), ("t3", 1), ("zr", p)], writes=[("zr", p)])

            def q3():
                def ftq(e):
                    ins = None
                    for h in range(4):
                        ins = e.transpose(out=ptr[0:96, h * 128:(h + 1) * 128], in_=zr[p][:, h * 96:(h + 1) * 96], identity=ident[:])
                    return ins
                P.op("pe", ftq, reads=[("zr", p), "ident"], writes=["pz2"])
                P.op("dve", lambda e: e.tensor_copy(out=hT[0:96, 0:4, tsl], in_=ptr[0:96, 0:512].rearrange("p (h t) -> p h t", h=4)),
                     reads=["pz2"], writes=[("qC", j, tt) for j in range(4)] + hT_cells(range(NT)))
            return [l1, l2, l3, q1, q2, q3]

        def units_Ck():
            units = []

            def ukr():
                for h in range(4):
                    P.op("dve", lambda e, h=h: e.tensor_copy(out=kc[64:96, h, :], in_=cl[64:96, 1, :]),
                         reads=[("cl", 3, t) for t in range(NT)], writes=[("kc", h, t) for t in range(NT)])
            units.append(ukr)
            for h in range(4):
                for tc in range(4):
                    def uk(h=h, tc=tc):
                        csl = slice(tc * 512, (tc + 1) * 512)
                        P.op("pe", lambda e: e.matmul(pz[0:64, 512:1024], lhsT=wukv[:, h * 128:h * 128 + 64], rhs=cl[:, 2, csl],
                                                      start=True, stop=True),
                             reads=[("cl", 2, t) for t in range(4 * tc, 4 * tc + 4)] + ["wukv"], writes=["pz2"])
                        P.op("dve", lambda e: e.tensor_copy(out=kc[0:64, h, csl], in_=pz[0:64, 512:1024]),
                             reads=["pz2"], writes=[("kcn", h, tc)] + [("kc", h, t) for t in range(4 * tc, 4 * tc + 4)])
                    units.append(uk)
            return units

        def units_Vc(h0):
            units = []
            for tt in range(NT):
                def uv(tt=tt):
                    tsl = slice(tt * 128, (tt + 1) * 128)

                    def fv(e):
                        ins = None
                        for i in range(2):
                            h = h0 + i
                            ins = e.matmul(pz[:, i * 64:(i + 1) * 64], lhsT=cl[:, 2, tsl], rhs=wukv[:, h * 128 + 64:h * 128 + 128],
                                           start=True, stop=True)
                        return ins
                    P.op("pe", fv, reads=[("cl", 2, tt), "wukv"], writes=["pz"])
                    for i in range(2):
                        h = h0 + i
                        lo = 0 if h % 2 == 0 else 64
                        P.op("dve", lambda e, i=i, h=h, lo=lo: e.tensor_copy(out=vext[:, tt, h, lo:lo + 64], in_=pz[:, i * 64:(i + 1) * 64]),
                             reads=["pz"], writes=[("vext", h)])
                units.append(uv)
            return units

        def heads_C():
            heads = []
            for h in (2, 3, 0, 1):
                hf = h % 2
                heads.append(dict(q=hT[0:96, h, :], k=kc[0:96, h, :], vh=h, ohalf=hf, scale=float(96 ** -0.5),
                                  window=False, sink=None, ot=6 + h // 2, gslot=h // 2, evac="dve",
                                  rq=[("qC", h, t) for t in range(NT)] + hT_cells(range(NT)),
                                  rk=[("kcn", h, tc) for tc in range(4)] + [("kc", h, t) for t in range(NT)]))
            return heads

        def units_WoLoad(l):
            units = []
            WO = WOs[0]
            for j in range(8):
                def u(j=j):
                    st = stages[j % 2]
                    sk = ("stage", j % 2)
                    if j < 6:
                        mixb = 384 * (j // 3)
                        jj = j % 3
                        r0 = mixb + jj * 64
                        r1 = mixb + (jj + 3) * 64
                        P.dma(lambda e, sem: (e.dma_start(out=st[0:64, :], in_=wout_d[l, r0:r0 + 64, :]).then_inc(sem, 16),
                                              e.dma_start(out=st[64:128, :], in_=wout_d[l, r1:r1 + 64, :]).then_inc(sem, 16)),
                              2, sk, writes=[sk])
                    else:
                        r0 = 768 + (j - 6) * 128
                        DMA(st[:], wout_d[l, r0:r0 + 128, :], sk, writes=[sk])
                    P.op("pool", lambda e: e.tensor_copy(out=WO[:, j, :], in_=st[:]), reads=[sk], writes=Wc(0))
                units.append(u)
            return units

        def units_O(s, l, src, dst, last):
            oc = oT_cells_all()
            tiles = [_o_tile(s, src, dst, last, oc, tt, tt % 2) for tt in range(NT)]
            return skew(tiles)

        def _o_tile(s, src, dst, last, oc, tt, p):
            tsl = slice(tt * 128, (tt + 1) * 128)
            xb, xc = xring[tt % 4]

            def s1():
                DMA(xb[:], src[s, tsl, :], xc, reads=[("xs", s, tt)], writes=[xc])
                acc = pz if p == 0 else Sp[0]
                acells = ["pz", "pz2"] if p == 0 else [("S", 0)]
                WO = WOs[0]

                def fo(e):
                    ins = None
                    for n in range(2):
                        for j in range(8):
                            ins = e.matmul(acc[:, n * 512:(n + 1) * 512], lhsT=oT[:, j, tsl], rhs=WO[:, j, n * 512:(n + 1) * 512],
                                           start=(j == 0), stop=(j == 7))
                    return ins
                P.op("pe", fo, reads=Wc(0) + oc, writes=acells)
                P.op("dve", lambda e: e.tensor_tensor(out=xb[:], in0=xb[:], in1=acc[:], op=ALU.add),
                     reads=[xc] + acells, writes=[xc])

            def sq():
                if last:
                    P.op("pool", lambda e: e.memset(sm[p][:, 0:1], 0.0), writes=[("sm", p)])
                    P.op("act", lambda e: e.activation(out=Sp[1][:], in_=xb[:], func=AF.Square, accum_out=sm[p][:, 0:1]),
                         reads=[xc, ("sm", p)], writes=[("S", 1), ("sm", p)])
                    rsqrt_small(0, 1, 1.0 / D, p)

            def s2():
                if last:
                    P.op("dve", lambda e: e.scalar_tensor_tensor(out=xb[:], in0=xb[:], scalar=sm2[p][:, 0:1], in1=gx[:],
                                                                 op0=ALU.mult, op1=ALU.mult),
                         reads=[xc, ("sm2", p, 0), "gx"], writes=[xc])
                DMA(dst[s, tsl, :], xb[:], ("xo", tt % 4), reads=[xc], writes=[("xs", s, tt)])
            return [s1, sq, s2] if last else [s1, s2]

        iters = [(s, l) for s in range(nseq) for l in range(depth)]
        for it, (s, l) in enumerate(iters):
            src = x_d if l == 0 else xs_d
            last = (l == depth - 1)
            dst = out_d if last else xs_d
            DMA(gx[:], gx_d[l], "gx", writes=["gx"])
            DMA(gqk[:], gqk_d[l], "gqk", writes=["gqk"])
            DMA(gcq[:], gcq_d[l], "gcq", writes=["gcq"])
            DMA(gckv[:], gckv_d[l], "gckv", writes=["gckv"])
            DMA(sinkt[:], sink_d[l], "sink", writes=["sink"])
            P.op("act", lambda e: e.activation(out=es_t[:], in_=sinkt[:], func=AF.Exp), reads=["sink"], writes=["es"])

            lb, tb, gb = units_projAB(s, l, 1)
            la, ta, ga = units_projAB(s, l, 0)
            if it == 0:
                seq(lb)
            zipped = []
            for i, u in enumerate(tb):
                zipped.append(u)
                if i < len(gb):
                    zipped.append(gb[i])
            zipped += gb[len(tb):]
            inter(units_N(s, l, src) + zipped, la)
            ub = units_attention_pairs(pairs_AB(1))
            nb = (len(ub) * 3) // 4
            inter(ub[:nb], ta)
            seq(ub[nb:])
            seq(ga)
            c_lat, c_q = units_projC1(s, l)
            ua = units_attention_pairs(pairs_AB(0))
            na = (len(ua) * 4) // 5
            inter(ua[:na], c_lat + skew(gate_tiles(2, [6, 7], 0)) + c_q + units_Ck() + units_Vc(2))
            seq(ua[na:])
            side = units_WoLoad(l)
            if it + 1 < len(iters):
                side = side + units_projAB(iters[it + 1][0], iters[it + 1][1], 1)[0]
            uc = units_attention(heads_C())
            inter(uc[:56], units_Vc(0))
            inter(uc[56:], side)
            if last:
                DMA(gx[:], gf_d, "gx", writes=["gx"])
            seq(units_O(s, l, src, dst, last))
        P.emit(final_wait_keys=[("xo", i) for i in range(4)])
    return nc


def _rope_tab(pos, dim):
    inv = (np.float32(10000.0) ** (-np.arange(0, dim, 2, dtype=np.float32) / np.float32(dim))).astype(np.float32)
    ang = pos.astype(np.float32)[:, None] * inv[None, :]
    ang = np.concatenate([ang, ang], axis=-1).astype(np.float32)
    cos = np.cos(ang).astype(np.float32)
    sin = np.sin(ang).astype(np.float32)
    half = dim // 2
    sin_signed = np.concatenate([-sin[:, :half], sin[:, half:]], axis=-1)
    return cos, sin_signed


def _constants():
    t = np.arange(S)
    cr, sr = _rope_tab(t // 64, 32)
    cc, sc = _rope_tab(t % 64, 32)
    tabA = np.concatenate([cr, cc, sr, sc], axis=-1)
    cb, sbn = _rope_tab(t, 64)
    tabB = np.concatenate([cb, sbn], axis=-1)
    ccm, scm = _rope_tab(t, 32)
    tabC = np.concatenate([ccm, scm], axis=-1)
    b = np.arange(128)[:, None, None]
    r = np.arange(4)[None, :, None]
    a = np.arange(256)[None, None, :]
    mask = np.where(np.abs((r - 1) * 128 + b - a) <= 128, 0.0, -30000.0).astype(np.float32).reshape(128, 1024)
    return dict(tabA=np.ascontiguousarray(tabA.reshape(NT, 128, 128), dtype=np.float32),
                tabB=np.ascontiguousarray(tabB.reshape(NT, 128, 128), dtype=np.float32),
                tabC=np.ascontiguousarray(tabC.reshape(NT, 128, 64), dtype=np.float32),
                maskb=mask, ident=np.eye(128, dtype=np.float32))


_NC_CACHE = {}


def kernel(x, norm_g, w_in, a_q_norm, a_k_norm, b_sink, c_q_norm, c_kv_norm, c_w_uq, c_w_ukv, w_out, final_g):
    f = lambda a: np.ascontiguousarray(np.asarray(a), dtype=np.float32)
    x = f(x); norm_g = f(norm_g); w_in = f(w_in); w_out = f(w_out)
    a_q_norm = f(a_q_norm); a_k_norm = f(a_k_norm); b_sink = f(b_sink)
    c_q_norm = f(c_q_norm); c_kv_norm = f(c_kv_norm); c_w_uq = f(c_w_uq); c_w_ukv = f(c_w_ukv); final_g = f(final_g)
    depth = norm_g.shape[0]
    rep = lambda v: np.ascontiguousarray(np.broadcast_to(v[:, None, :], (v.shape[0], 128, v.shape[1])))
    gqk = np.concatenate([np.tile(a_q_norm, (1, 6)), np.tile(a_k_norm, (1, 2))], axis=1)
    shared = dict(w_in=w_in, w_out=w_out, c_w_uq=c_w_uq, c_w_ukv=c_w_ukv,
                  gx=rep(norm_g), gf=np.ascontiguousarray(np.broadcast_to(final_g[None, :], (128, D))),
                  gqk=rep(gqk), gcq=rep(c_q_norm), gckv=rep(c_kv_norm), sink=rep(b_sink))
    shared.update(_constants())
    if "nc" not in _NC_CACHE:
        _NC_CACHE["nc"] = build_program(depth, NSEQ)
    nc = _NC_CACHE["nc"]
    ncores = 8
    in_maps = []
    for c in range(ncores):
        m = dict(shared)
        m["x"] = np.ascontiguousarray(x[c * NSEQ:(c + 1) * NSEQ])
        in_maps.append(m)
    res = run_bass_kernel_spmd(nc, in_maps, core_ids=list(range(ncores)))
    out = np.concatenate([np.asarray(r["out"]) for r in res.results], axis=0)
    return out.astype(np.float32)
```

```python
import numpy as np
from contextlib import ExitStack
import concourse.bass as bass
import concourse.mybir as mybir
from concourse.bass_utils import run_bass_kernel_spmd

F32 = mybir.dt.float32
BF16 = mybir.dt.bfloat16
ALU = mybir.AluOpType
AF = mybir.ActivationFunctionType
AX = mybir.AxisListType

S = 2048
D = 1024
NT = 16
DEPTH = 2
NSEQ = 2
EPS = 1e-6
INC = 2656

ENGS = ("pe", "act", "dve", "pool", "sp")


class _Op:
    __slots__ = ("eng", "fn", "deps", "signal", "ticket", "dma", "semkey", "ndma", "dma_val", "waits", "idx")


class Prog:
    def __init__(self, nc):
        self.nc = nc
        self.ops = []
        self.last_writer = {}
        self.readers = {}
        self.dma_counts = {}

    def _deps(self, reads, writes):
        deps = set()
        for c in reads:
            w = self.last_writer.get(c)
            if w is not None:
                deps.add(w)
        for c in writes:
            w = self.last_writer.get(c)
            if w is not None:
                deps.add(w)
            for r in self.readers.get(c, ()):
                deps.add(r)
        return deps

    def _commit(self, idx, reads, writes):
        for c in reads:
            self.readers.setdefault(c, []).append(idx)
        for c in writes:
            self.last_writer[c] = idx
            self.readers[c] = []

    def op(self, eng, fn, reads=(), writes=()):
        o = _Op()
        o.eng = eng; o.fn = fn; o.dma = False; o.signal = False; o.ticket = None
        o.idx = len(self.ops)
        reads = list(reads); writes = list(writes)
        o.deps = self._deps(reads, writes)
        self.ops.append(o)
        self._commit(o.idx, reads, writes)
        return o.idx

    def dma(self, fn, ndma, semkey, reads=(), writes=(), eng="sp"):
        o = _Op()
        o.eng = eng; o.fn = fn; o.dma = True; o.signal = False; o.ticket = None
        o.idx = len(self.ops)
        o.semkey = semkey; o.ndma = ndma
        self.dma_counts[semkey] = self.dma_counts.get(semkey, 0) + ndma
        o.dma_val = self.dma_counts[semkey] * 16
        reads = list(reads); writes = list(writes)
        o.deps = self._deps(reads, writes)
        self.ops.append(o)
        self._commit(o.idx, reads, writes)
        return o.idx

    def emit(self, final_wait_keys=()):
        nc = self.nc
        ops = self.ops
        for o in ops:
            for j in o.deps:
                d = ops[j]
                if d.dma:
                    continue
                if d.eng == o.eng and o.eng == "pe":
                    continue
                d.signal = True
        cnt = {e: 0 for e in ENGS}
        for o in ops:
            if o.signal and not o.dma:
                cnt[o.eng] += 1
                o.ticket = cnt[o.eng]
        with ExitStack() as es:
            sems = {e: es.enter_context(nc.semaphore("s_" + e)) for e in ("pe", "act", "dve", "pool")}
            dsems = {}
            for k in self.dma_counts:
                dsems[k] = es.enter_context(nc.semaphore("d_%d" % len(dsems)))
            waited = {e: {} for e in ENGS}
            for o in ops:
                w = {}
                for j in o.deps:
                    d = ops[j]
                    if d.dma:
                        key = ("d", d.semkey); val = d.dma_val
                    else:
                        if d.eng == o.eng and o.eng == "pe":
                            continue
                        key = ("e", d.eng); val = d.ticket
                    if w.get(key, 0) < val:
                        w[key] = val
                o.waits = []
                for key, val in w.items():
                    if waited[o.eng].get(key, 0) >= val:
                        continue
                    waited[o.eng][key] = val
                    sem = dsems[key[1]] if key[0] == "d" else sems[key[1]]
                    o.waits.append((sem, val))
            block = es.enter_context(nc.Block())

            def run(engname, eng):
                for o in ops:
                    if o.eng != engname:
                        continue
                    for sem, val in o.waits:
                        eng.wait_ge(sem, val)
                    if o.dma:
                        o.fn(eng, dsems[o.semkey])
                    else:
                        ins = o.fn(eng)
                        if o.signal:
                            ins.then_inc(sems[engname], 1)
                if engname == "sp":
                    for k in final_wait_keys:
                        eng.wait_ge(dsems[k], self.dma_counts[k] * 16)

            @block.tensor
            def _(eng):
                run("pe", eng)

            @block.scalar
            def _(eng):
                run("act", eng)

            @block.vector
            def _(eng):
                run("dve", eng)

            @block.gpsimd
            def _(eng):
                run("pool", eng)

            @block.sync
            def _(eng):
                run("sp", eng)


def build_program(depth=DEPTH, nseq=NSEQ):
    nc = bass.Bass("TRN2", target_bir_lowering=False)

    def din(name, shape):
        return nc.dram_tensor(name, list(shape), F32, kind="ExternalInput").ap()

    x_d = din("x", [nseq, S, D])
    win_d = din("w_in", [depth, D, INC])
    wout_d = din("w_out", [depth, D, D])
    wuq_d = din("c_w_uq", [depth, 192, 384])
    wukv_d = din("c_w_ukv", [depth, 128, 512])
    gx_d = din("gx", [depth, 128, D])
    gf_d = din("gf", [128, D])
    gqk_d = din("gqk", [depth, 128, 512])
    gcq_d = din("gcq", [depth, 128, 192])
    gckv_d = din("gckv", [depth, 128, 128])
    sink_d = din("sink", [depth, 128, 6])
    tabA_d = din("tabA", [NT, 128, 128])
    tabB_d = din("tabB", [NT, 128, 128])
    tabC_d = din("tabC", [NT, 128, 64])
    mask_d = din("maskb", [128, 1024])
    ident_d = din("ident", [128, 128])
    out_d = nc.dram_tensor("out", [nseq, S, D], F32, kind="ExternalOutput").ap()
    xs_d = out_d

    with ExitStack() as es:
        def sb(name, shape, dt=F32):
            return es.enter_context(nc.sbuf_tensor(name, list(shape), dt))

        def ps(name, shape, dt=F32):
            return es.enter_context(nc.psum_tensor(name, list(shape), dt))

        hT = sb("hT", [128, 8, S], BF16)
        oT = sb("oT", [128, 8, S], BF16)
        warena = [sb("warena%d" % i, [128, 8 * 1024], BF16) for i in range(2)]
        stages = [sb("stage%d" % i, [128, 1024]) for i in range(2)]
        stage = stages[0]
        qk = sb("qk", [128, 4, S], BF16)
        kc = sb("kc", [128, 4, S], BF16)
        cl = sb("cl", [128, 3, S], BF16)
        vext = sb("vext", [128, NT, 4, 128], BF16)
        tab = [sb("tab%d" % i, [128, 128]) for i in range(2)]
        xt = [sb("xt%d" % i, [128, D]) for i in range(2)]
        zs = [sb("zs%d" % i, [128, 512]) for i in range(2)]
        t1s = sb("t1s", [128, 512])
        t2s = sb("t2s", [128, 512])
        t3s = sb("t3s", [128, 512])
        zr = [sb("zr%d" % i, [128, 512], BF16) for i in range(2)]
        Pb = [sb("P%d" % i, [128, 1024], BF16) for i in range(3)]
        hbs = Pb[2]
        Rt = sb("Rt", [128, 512])
        tf = sb("tf", [128, 512])
        gx = sb("gx_t", [128, D])
        gqk = sb("gqk_t", [128, 512])
        gcq = sb("gcq_t", [128, 192])
        gckv = sb("gckv_t", [128, 128])
        sinkt = sb("sink_t", [128, 6])
        es_t = sb("es_t", [128, 6])
        maskb = sb("maskb_t", [128, 4, 256], BF16)
        ident = sb("ident_t", [128, 128], BF16)
        wuq0 = sb("wuq0", [128, 384], BF16)
        wuq1 = sb("wuq1", [128, 384], BF16)
        wukv = sb("wukv", [128, 512], BF16)
        sm = [sb("sm%d" % i, [128, 16]) for i in range(2)]
        sm2 = [sb("sm2_%d" % i, [128, 16]) for i in range(2)]
        epst = sb("epst", [128, 1])
        krp = [sb("krp%d" % i, [128, 128], BF16) for i in range(2)]

        pz = ps("pz", [128, 1024])
        ptr = pz[:, 512:1024].bitcast(BF16)
        Sp = [ps("S%d" % i, [128, 1024]) for i in range(2)]
        Op2 = ps("Op", [128, 1024])
        Op = Op2[:, 0:512]

        P = Prog(nc)

        def DMA(out, in_, semkey, reads=(), writes=()):
            P.dma(lambda e, s: e.dma_start(out=out, in_=in_).then_inc(s, 16), 1, semkey, reads=reads, writes=writes)

        def seq(units):
            for u in units:
                u()

        def inter(main, side):
            nm, ns = len(main), len(side)
            si = 0
            for i, u in enumerate(main):
                u()
                target = ((i + 1) * ns) // nm
                while si < target:
                    side[si]()
                    si += 1
            while si < ns:
                side[si]()
                si += 1

        def skew(tiles):
            nt = len(tiles)
            ns = max(len(t) for t in tiles)
            units = []
            for step in range(nt + ns - 1):
                fs = []
                for st in range(ns):
                    t = step - st
                    if 0 <= t < nt and st < len(tiles[t]):
                        fs.append(tiles[t][st])
                units.append(lambda fs=fs: [f() for f in fs])
            return units

        DMA(stage[:, 0:128], ident_d, ("stage", 0), writes=[("stage", 0)])
        P.op("pool", lambda e: e.tensor_copy(out=ident[:], in_=stage[:, 0:128]), reads=[("stage", 0)], writes=["ident"])
        DMA(stage[:, 0:1024], mask_d, ("stage", 0), writes=[("stage", 0)])
        P.op("pool", lambda e: e.tensor_copy(out=maskb[:].rearrange("p r a -> p (r a)"), in_=stage[:, 0:1024]),
             reads=[("stage", 0)], writes=["maskb"])
        P.op("pool", lambda e: e.memset(epst[:], EPS), writes=["epst"])
        for i in range(2):
            P.op("pool", lambda e, i=i: e.memset(krp[i][:], 0.0), writes=[("krp", i)])
        for h in range(4):
            lo = 64 if h % 2 == 0 else 0
            P.op("pool", lambda e, h=h, lo=lo: e.memset(vext[:, :, h, lo:lo + 64], 1.0), writes=[("vones", h)])

        def rsqrt_small(col0, n, scale, p, extra_reads=()):
            src = sm[p][:, col0:col0 + n]
            dst = sm2[p][:, col0:col0 + n]
            P.op("act", lambda e: e.activation(out=dst, in_=src, func=AF.Ln, bias=epst[:], scale=scale),
                 reads=[("sm", p), "epst"] + list(extra_reads), writes=[("sm2", p, col0)])
            P.op("act", lambda e: e.activation(out=dst, in_=dst, func=AF.Exp, scale=-0.5),
                 reads=[("sm2", p, col0)], writes=[("sm2", p, col0)])

        def rope(eng, x, H, G, Dh, tcos, tsin, out_a, out_b, rcells, p):
            R = G * 2 * Dh
            x3 = x.rearrange("p (h r) -> p h r", h=H)
            P.op(eng, lambda e: e.tensor_tensor(out=out_a.rearrange("p (h r) -> p h r", h=H), in0=x3,
                                                in1=tcos.unsqueeze(1).to_broadcast([128, H, R]), op=ALU.mult),
                 reads=list(rcells) + [("tab", p)], writes=[("t2",)])
            x5 = x.rearrange("p (h g t d) -> p h g t d", h=H, g=G, t=2, d=Dh)
            b5 = out_b.rearrange("p (h g t d) -> p h g t d", h=H, g=G, t=2, d=Dh)
            s4 = tsin.rearrange("p (g t d) -> p g t d", g=G, t=2, d=Dh)
            for t in range(2):
                P.op(eng, lambda e, t=t: e.tensor_tensor(
                    out=b5[:, :, :, t, :], in0=x5[:, :, :, 1 - t, :],
                    in1=s4[:, :, t, :].unsqueeze(1).to_broadcast([128, H, G, Dh]), op=ALU.mult),
                    reads=list(rcells) + [("tab", p)], writes=[("t3", t)])

        WAs = [w[:, 0:5120].rearrange("p (c n) -> p c n", c=8) for w in warena]
        WGs = [w[:, 5120:8192].rearrange("p (c n) -> p c n", c=8) for w in warena]
        WOs = [w[:].rearrange("p (j n) -> p j n", j=8) for w in warena]

        def Wc(a):
            return [("W", a, c) for c in range(8)]

        def hT_cells(tts):
            return [("hT", t) for t in tts]

        def units_load_win(l, col0, ncols, casts, a):
            units = []
            for c in range(8):
                def u(c=c):
                    st = stages[c % 2]
                    sk = ("stage", c % 2)
                    DMA(st[:, 0:ncols], win_d[l, c * 128:(c + 1) * 128, col0:col0 + ncols], sk, writes=[sk])
                    for fn in casts:
                        P.op("pool", lambda e, c=c, fn=fn, st=st: fn(e, c, st), reads=[sk], writes=[("W", a, c)])
                units.append(u)
            return units

        def gate_tiles(ncol_groups, slots, a):
            tiles = []
            k = 0
            for gi in range(ncol_groups):
                for tc in range(4):
                    tiles.append(_gate_tile(gi, tc, slots[gi], a, k % 2))
                    k += 1
            return tiles

        def oTc(ot, ohs, col0, ncols):
            return [("oT", ot, oh, q) for oh in ohs for q in range(col0 // 256, (col0 + ncols) // 256)]

        def _gate_tile(gi, tc, slot, a, bk):
            csl = slice(tc * 512, (tc + 1) * 512)

            pzb = pz[:, bk * 512:(bk + 1) * 512]
            pcell = "pz" if bk == 0 else "pz2"

            def g1():
                def f(e):
                    ins = None
                    for c in range(8):
                        ins = e.matmul(pzb, lhsT=WGs[a][:, c, gi * 128:(gi + 1) * 128], rhs=hT[:, c, csl],
                                       start=(c == 0), stop=(c == 7))
                    return ins
                P.op("pe", f, reads=Wc(a) + hT_cells(range(4 * tc, 4 * tc + 4)), writes=[pcell])
                P.op("act", lambda e: e.activation(out=oT[:, slot, csl], in_=pzb, func=AF.Silu),
                     reads=[pcell], writes=oTc(slot, (0, 1), tc * 512, 512))
            return [g1]

        def units_attention(heads):
            steps = []
            for hd in heads:
                CQ = 256 if hd["window"] else 512
                for qc in range(S // CQ):
                    if hd["window"]:
                        kts = [kt for kt in range(2 * qc - 1, 2 * qc + 3) if 0 <= kt < NT]
                        groups = [kts]
                    else:
                        kts = list(range(NT))
                        groups = [kts[i:i + 2] for i in range(0, NT, 2)]
                    for gi, g in enumerate(groups):
                        steps.append(dict(hd=hd, qc=qc, CQ=CQ, kts=g, first=(gi == 0), last=(gi == len(groups) - 1)))

            def emit_qk(i, st):
                hd = st["hd"]; CQ = st["CQ"]; qc = st["qc"]; sbuf = Sp[i % 2]
                win = hd["window"]
                r0 = (st["kts"][0] - (2 * qc - 1)) if win else 0

                def f(e):
                    ins = None
                    for n, kt in enumerate(st["kts"]):
                        ins = e.matmul(sbuf[:, n * CQ:(n + 1) * CQ], lhsT=hd["k"][:, kt * 128:(kt + 1) * 128],
                                       rhs=hd["q"][:, qc * CQ:(qc + 1) * CQ], start=True, stop=(not win))
                        if win:
                            ins = e.matmul(sbuf[:, n * CQ:(n + 1) * CQ], lhsT=ident[:], rhs=maskb[:, r0 + n, :],
                                           start=False, stop=True)
                    return ins
                P.op("pe", f, reads=hd["rq"] + hd["rk"] + (["maskb", "ident"] if win else []), writes=[("S", i % 2)])

            def emit_exp(i, st):
                hd = st["hd"]; CQ = st["CQ"]; n = len(st["kts"]); sbuf = Sp[i % 2]; pb = Pb[i % 3]
                P.op("act", lambda e: e.activation(out=pb[:, 0:n * CQ], in_=sbuf[:, 0:n * CQ], func=AF.Exp, scale=hd["scale"]),
                     reads=[("S", i % 2)], writes=[("P", i % 3)])

            def emit_pv(i, st):
                hd = st["hd"]; CQ = st["CQ"]; qc = st["qc"]; pb = Pb[i % 3]
                nk = len(st["kts"])

                def f(e):
                    ins = None
                    for n, kt in enumerate(st["kts"]):
                        ins = e.matmul(Op[:, 0:CQ], lhsT=vext[:, kt, hd["vh"], :], rhs=pb[:, n * CQ:(n + 1) * CQ],
                                       start=(st["first"] and n == 0), stop=(st["last"] and n == nk - 1))
                    return ins
                P.op("pe", f, reads=[("P", i % 3), ("vext", hd["vh"]), ("vones", hd["vh"])], writes=["Op"])
                if st["last"]:
                    oh = hd["ohalf"]
                    po = slice(64 * oh, 64 * oh + 64)
                    psl = slice(64 * (1 - oh), 64 * (1 - oh) + 64)
                    co = (qc % 2) * 256 if CQ == 256 else 0
                    tcs = [("tf", oh, k) for k in range(co // 256, (co + CQ) // 256)]
                    rcs = [("Rt", oh, k) for k in range(co // 256, (co + CQ) // 256)]
                    if hd["sink"] is not None:
                        sc = hd["sink"]
                        lnf = lambda e: e.activation(out=Rt[po, co:co + CQ], in_=Op[psl, 0:CQ], func=AF.Ln, bias=es_t[po, sc:sc + 1])
                        lnr = ["Op", "es"]
                    else:
                        lnf = lambda e: e.activation(out=Rt[po, co:co + CQ], in_=Op[psl, 0:CQ], func=AF.Ln)
                        lnr = ["Op"]
                    if hd.get("evac", "act") == "act":
                        P.op("act", lambda e: e.activation(out=tf[po, co:co + CQ], in_=Op[po, 0:CQ], func=AF.Copy), reads=["Op"], writes=tcs)
                    else:
                        P.op("dve", lambda e: e.tensor_copy(out=tf[po, co:co + CQ], in_=Op[po, 0:CQ]), reads=["Op"], writes=tcs + ["oplock"])
                        lnr = lnr + ["oplock"]
                    P.op("act", lnf, reads=lnr, writes=rcs)

            def emit_fin2(st):
                hd = st["hd"]; CQ = st["CQ"]; qc = st["qc"]
                oh = hd["ohalf"]
                po = slice(64 * oh, 64 * oh + 64)
                co = (qc % 2) * 256 if CQ == 256 else 0
                tcs = [("tf", oh, k) for k in range(co // 256, (co + CQ) // 256)]
                rcs = [("Rt", oh, k) for k in range(co // 256, (co + CQ) // 256)]
                R = Rt[po, co:co + CQ]
                T = tf[po, co:co + CQ]
                P.op("act", lambda e: e.activation(out=R, in_=R, func=AF.Exp, scale=-1.0), reads=rcs, writes=rcs)
                P.op("dve", lambda e: e.tensor_tensor(out=T, in0=T, in1=R, op=ALU.mult), reads=tcs + rcs, writes=tcs)
                cs = slice(qc * CQ, (qc + 1) * CQ)
                occ = oTc(hd["ot"], (oh,), qc * CQ, CQ)
                P.op("pool", lambda e: e.tensor_tensor(out=oT[po, hd["ot"], cs], in0=T, in1=oT[po, hd["ot"], cs], op=ALU.mult),
                     reads=tcs + occ, writes=occ)

            units = []
            n = len(steps)
            for i in range(n + 3):
                def u(i=i):
                    if i < n:
                        emit_qk(i, steps[i])
                        emit_exp(i, steps[i])
                    if 3 <= i and steps[i - 3]["last"]:
                        emit_fin2(steps[i - 3])
                    if 2 <= i <= n + 1:
                        emit_pv(i - 2, steps[i - 2])
                units.append(u)
            return units

        def units_attention_pairs(pairs):
            steps = []
            for pr in pairs:
                CQ = 256 if pr["window"] else 512
                for qc in range(S // CQ):
                    if pr["window"]:
                        kts = [kt for kt in range(2 * qc - 1, 2 * qc + 3) if 0 <= kt < NT]
                        groups = [kts[0:2], kts[2:4]]
                    else:
                        kts = list(range(NT))
                        groups = [[kt] for kt in kts]
                    groups = [g for g in groups if g]
                    for gi, g in enumerate(groups):
                        steps.append(dict(pr=pr, qc=qc, CQ=CQ, kts=g, kfirst=kts[0], klast=kts[-1],
                                          last=(gi == len(groups) - 1),
                                          blocks=[(kt, hf) for kt in g for hf in (0, 1)]))

            def col(st, n, hf):
                return hf * 512 + n * st["CQ"]

            def emit_qk(i, st):
                pr = st["pr"]; CQ = st["CQ"]; qc = st["qc"]; sbuf = Sp[i % 2]
                win = pr["window"]

                def f(e):
                    ins = None
                    for n, kt in enumerate(st["kts"]):
                        for hf in (0, 1):
                            prt = slice(64 * hf, 64 * hf + 64)
                            c0 = col(st, n, hf)
                            ins = e.matmul(sbuf[:, c0:c0 + CQ], lhsT=pr["k"][prt, kt * 128:(kt + 1) * 128],
                                           rhs=pr["q"][prt, qc * CQ:(qc + 1) * CQ], start=True, stop=(not win))
                        if win:
                            r = kt - (2 * qc - 1)
                            for hf in (0, 1):
                                c0 = col(st, n, hf)
                                ins = e.matmul(sbuf[:, c0:c0 + CQ], lhsT=ident[:], rhs=maskb[:, r, :], start=False, stop=True)
                    return ins
                P.op("pe", f, reads=pr["rq"] + pr["rk"] + (["maskb", "ident"] if win else []), writes=[("S", i % 2)])

            def emit_exp(i, st):
                pr = st["pr"]; CQ = st["CQ"]; nk = len(st["kts"]); sbuf = Sp[i % 2]; pb = Pb[i % 3]
                if nk * CQ == 512:
                    P.op("act", lambda e: e.activation(out=pb[:, 0:1024], in_=sbuf[:, 0:1024], func=AF.Exp, scale=pr["scale"]),
                         reads=[("S", i % 2)], writes=[("P", i % 3)])
                else:
                    w = nk * CQ
                    for hf in (0, 1):
                        P.op("act", lambda e, hf=hf: e.activation(out=pb[:, hf * 512:hf * 512 + w], in_=sbuf[:, hf * 512:hf * 512 + w],
                                                                   func=AF.Exp, scale=pr["scale"]),
                             reads=[("S", i % 2)], writes=[("P", i % 3)])

            def emit_pv(i, st):
                pr = st["pr"]; CQ = st["CQ"]; qc = st["qc"]; pb = Pb[i % 3]
                vb = pr["vbase"]

                def f(e):
                    ins = None
                    for n, kt in enumerate(st["kts"]):
                        for hf in (0, 1):
                            c0 = col(st, n, hf)
                            ins = e.matmul(Op2[:, hf * 512:hf * 512 + CQ], lhsT=vext[:, kt, vb + hf, :], rhs=pb[:, c0:c0 + CQ],
                                           start=(kt == st["kfirst"]), stop=(kt == st["klast"]))
                    return ins
                P.op("pe", f, reads=[("P", i % 3), ("vext", vb), ("vext", vb + 1), ("vones", vb), ("vones", vb + 1)], writes=["Op"])
                if st["last"]:
                    co = (qc % 2) * 256 if CQ == 256 else 0
                    kr = range(co // 256, (co + CQ) // 256)
                    for hf in (0, 1):
                        po = slice(64 * hf, 64 * hf + 64)
                        psl = slice(64 * (1 - hf), 64 * (1 - hf) + 64)
                        acc = Op2[:, hf * 512:hf * 512 + CQ]
                        tcs = [("tf", hf, k) for k in kr]
                        rcs = [("Rt", hf, k) for k in kr]
                        P.op("act", lambda e, po=po, acc=acc: e.activation(out=tf[po, co:co + CQ], in_=acc[po, :], func=AF.Copy),
                             reads=["Op"], writes=tcs)
                        if pr["sinks"] is not None:
                            sc = pr["sinks"][hf]
                            P.op("act", lambda e, po=po, psl=psl, acc=acc, sc=sc: e.activation(
                                out=Rt[po, co:co + CQ], in_=acc[psl, :], func=AF.Ln, bias=es_t[po, sc:sc + 1]),
                                reads=["Op", "es"], writes=rcs)
                        else:
                            P.op("act", lambda e, po=po, psl=psl, acc=acc: e.activation(out=Rt[po, co:co + CQ], in_=acc[psl, :], func=AF.Ln),
                                 reads=["Op"], writes=rcs)

            def emit_fin2(st):
                pr = st["pr"]; CQ = st["CQ"]; qc = st["qc"]
                co = (qc % 2) * 256 if CQ == 256 else 0
                kr = range(co // 256, (co + CQ) // 256)
                tcs = [("tf", hf, k) for hf in (0, 1) for k in kr]
                rcs = [("Rt", hf, k) for hf in (0, 1) for k in kr]
                R = Rt[:, co:co + CQ]
                T = tf[:, co:co + CQ]
                P.op("act", lambda e: e.activation(out=R, in_=R, func=AF.Exp, scale=-1.0), reads=rcs, writes=rcs)
                P.op("dve", lambda e: e.tensor_tensor(out=T, in0=T, in1=R, op=ALU.mult), reads=tcs + rcs, writes=tcs)
                cs = slice(qc * CQ, (qc + 1) * CQ)
                occ = oTc(pr["ot"], (0, 1), qc * CQ, CQ)
                P.op("pool", lambda e: e.tensor_tensor(out=oT[:, pr["ot"], cs], in0=T, in1=oT[:, pr["ot"], cs], op=ALU.mult),
                     reads=tcs + occ, writes=occ)

            units = []
            n = len(steps)
            for i in range(n + 3):
                def u(i=i):
                    if i < n:
                        emit_qk(i, steps[i])
                        emit_exp(i, steps[i])
                    if 3 <= i and steps[i - 3]["last"]:
                        emit_fin2(steps[i - 3])
                    if 2 <= i <= n + 1:
                        emit_pv(i - 2, steps[i - 2])
                units.append(u)
            return units

        def pairs_AB(mix):
            isA = (mix == 0)
            srcq = qk if isA else kc
            dname = "qk" if isA else "kc"
            prs = []
            for j in range(3):
                prs.append(dict(q=srcq[:, j, :], k=srcq[:, 3, :], vbase=(0 if isA else 2), scale=0.125, window=(not isA),
                                sinks=(None if isA else (j, j + 3)), ot=3 * mix + j,
                                rq=[(dname, j, t) for t in range(NT)], rk=[(dname, 3, t) for t in range(NT)]))
            return prs

        def oT_cells_all():
            cells = []
            for j in range(8):
                cells += oTc(j, (0, 1), 0, S)
            return cells

        def transposes(src_fn, n, reads, dst_ap, dst_cells, rows=128):
            def f(e):
                ins = None
                for i in range(n):
                    ins = e.transpose(out=ptr[0:rows, i * 128:(i + 1) * 128], in_=src_fn(i), identity=ident[:])
                return ins
            P.op("pe", f, reads=list(reads) + ["ident"], writes=["pz2"])
            P.op("dve", lambda e: e.tensor_copy(out=dst_ap, in_=ptr[0:rows, 0:n * 128].rearrange("p (i t) -> p i t", i=n)),
                 reads=["pz2"], writes=dst_cells)

        xring = [(xt[0], ("xt", 0)), (xt[1], ("xt", 1)), (stages[0], ("stage", 0)), (stages[1], ("stage", 1))]

        def units_N(s, l, src):
            tiles = []
            for tt in range(NT):
                tiles.append(_n_tile(s, l, src, tt, tt % 2))
            return skew(tiles)

        def _n_tile(s, l, src, tt, p):
            xb, xc = xring[tt % 2]

            def s1():
                DMA(xb[:], src[s, tt * 128:(tt + 1) * 128, :], xc, reads=[("xs", s, tt)], writes=[xc])
                P.op("pool", lambda e: e.memset(sm[p][:, 0:1], 0.0), writes=[("sm", p)])
                P.op("act", lambda e: e.activation(out=Sp[1][:], in_=xb[:], func=AF.Square, accum_out=sm[p][:, 0:1]),
                     reads=[xc, ("sm", p)], writes=[("S", 1), ("sm", p)])
                rsqrt_small(0, 1, 1.0 / D, p)

            def s2():
                P.op("dve", lambda e: e.scalar_tensor_tensor(out=hbs[:], in0=xb[:], scalar=sm2[p][:, 0:1], in1=gx[:],
                                                             op0=ALU.mult, op1=ALU.mult),
                     reads=[xc, ("sm2", p, 0), "gx"], writes=[("P", 2)])
                transposes(lambda i: hbs[:, i * 128:(i + 1) * 128], 8, [("P", 2)],
                           hT[:, :, tt * 128:(tt + 1) * 128], [("hT", tt)])
            return [s1, s2]

        def units_projAB(s, l, mix):
            isA = (mix == 0)
            c0 = 1024 * mix
            dstq = qk if isA else kc
            dname = "qk" if isA else "kc"
            vbase = 0 if isA else 2

            a = mix
            WA = WAs[a]
            WG = WGs[a]

            def cast_q(e, c, st):
                return e.tensor_copy(out=WA[:, c, 0:384].rearrange("p (j hf d) -> p j hf d", j=3, hf=2),
                                     in_=st[:, 0:384].rearrange("p (hf j d) -> p j hf d", hf=2, j=3))

            def cast_kv(e, c, st):
                return e.tensor_copy(out=WA[:, c, 384:640], in_=st[:, 384:640])

            def cast_g(e, c, st):
                return e.tensor_copy(out=WG[:, c, :].rearrange("p (j hf d) -> p j hf d", j=3, hf=2),
                                     in_=st[:, 640:1024].rearrange("p (hf j d) -> p j hf d", hf=2, j=3))
            lunits = units_load_win(l, c0, 1024, [cast_q, cast_kv, cast_g], a)
            tabd = tabA_d if isA else tabB_d
            tiles = []
            for tt in range(NT):
                tiles.append(_ab_tile(isA, tabd, dstq, dname, vbase, tt, tt % 2, a))
            return lunits, skew(tiles), skew(gate_tiles(3, [3 * mix + j for j in range(3)], a))

        def _ab_tile(isA, tabd, dstq, dname, vbase, tt, p, a):
            tsl = slice(tt * 128, (tt + 1) * 128)
            WA = WAs[a]
            if (not isA) and p == 1:
                acc = Sp[1]
                c_qk = [("S", 1)]
                c_v = [("S", 1)]
            else:
                acc = pz
                c_qk = ["pz"]
                c_v = ["pz2"]

            def s1():
                def f(e):
                    ins = None
                    for c in range(8):
                        ins = e.matmul(acc[:, 0:512], lhsT=hT[:, c, tsl], rhs=WA[:, c, 0:512], start=(c == 0), stop=(c == 7))
                    for c in range(8):
                        ins = e.matmul(acc[:, 512:640], lhsT=hT[:, c, tsl], rhs=WA[:, c, 512:640], start=(c == 0), stop=(c == 7))
                    return ins
                P.op("pe", f, reads=Wc(a) + [("hT", tt)], writes=list(set(c_qk + c_v)))
                DMA(tab[p][:], tabd[tt], ("tab", p), writes=[("tab", p)])
                P.op("dve", lambda e: e.tensor_copy(out=zs[p][:, 0:512], in_=acc[:, 0:512]), reads=c_qk, writes=[("zs", p)])
                for kv in range(2):
                    lo = 0 if kv == 0 else 64
                    P.op("dve", lambda e, kv=kv, lo=lo: e.tensor_copy(out=vext[:, tt, vbase + kv, lo:lo + 64],
                                                                     in_=acc[:, 512 + kv * 64:576 + kv * 64]),
                         reads=c_v, writes=[("vext", vbase + kv)])
                if isA:
                    P.op("pool", lambda e: e.tensor_tensor(out=t1s[:], in0=zs[p][:, 0:512], in1=zs[p][:, 0:512], op=ALU.mult),
                         reads=[("zs", p)], writes=[("t1",)])
                    P.op("dve", lambda e: e.tensor_reduce(out=sm[p][:, 0:8], in_=t1s[:].rearrange("p (h d) -> p h d", d=64),
                                                          axis=AX.X, op=ALU.add), reads=[("t1",)], writes=[("sm", p)])

            def s2():
                if isA:
                    rsqrt_small(0, 8, 1.0 / 64, p)
                    P.op("dve", lambda e: e.tensor_tensor(out=t1s[:].rearrange("p (h d) -> p h d", d=64),
                                                          in0=zs[p][:, 0:512].rearrange("p (h d) -> p h d", d=64),
                                                          in1=sm2[p][:, 0:8].unsqueeze(2).to_broadcast([128, 8, 64]), op=ALU.mult),
                         reads=[("zs", p), ("sm2", p, 0)], writes=[("t1",)])
                    P.op("dve", lambda e: e.tensor_tensor(out=t1s[:], in0=t1s[:], in1=gqk[:], op=ALU.mult),
                         reads=[("t1",), "gqk"], writes=[("t1",)])
                    rope("pool", t1s[:], 8, 2, 16, tab[p][:, 0:64], tab[p][:, 64:128], t2s[:], t3s[:], [("t1",)], p)
                else:
                    rope("pool", zs[p][:, 0:512], 8, 1, 32, tab[p][:, 0:64], tab[p][:, 64:128], t2s[:], t3s[:], [("zs", p)], p)
                P.op("dve", lambda e: e.tensor_tensor(out=zr[p][:], in0=t2s[:], in1=t3s[:], op=ALU.add),
                     reads=[("t2",), ("t3", 0), ("t3", 1)], writes=[("zr", p)])

            def s3():
                transposes(lambda i: zr[p][:, i * 128:(i + 1) * 128], 4, [("zr", p)],
                           dstq[:, :, tsl], [(dname, j, tt) for j in range(4)])
            return [s1, s2, s3]

        def heads_AB(mix):
            isA = (mix == 0)
            srcq = qk if isA else kc
            dname = "qk" if isA else "kc"
            vbase = 0 if isA else 2
            heads = []
            for j in range(3):
                for hf in range(2):
                    h = j + 3 * hf
                    prt = slice(64 * hf, 64 * hf + 64)
                    heads.append(dict(q=srcq[prt, j, :], k=srcq[prt, 3, :], vh=vbase + hf, ohalf=hf, scale=0.125,
                                      window=(not isA), sink=(None if isA else h), ot=3 * mix + j, gslot=j,
                                      rq=[(dname, j, t) for t in range(NT)], rk=[(dname, 3, t) for t in range(NT)]))
            return heads

        def units_projC1(s, l):
            WA = WAs[0]
            WG = WGs[0]

            def cast_c(e, c, st):
                return e.tensor_copy(out=WA[:, c, 0:352], in_=st[:, 0:352])

            def cast_cg(e, c, st):
                return e.tensor_copy(out=WG[:, c, 0:256], in_=st[:, 352:608])
            units = units_load_win(l, 2048, 608, [cast_c, cast_cg], 0)

            def uw():
                sk = ("stage", 0)
                DMA(stage[:, 0:384], wuq_d[l, 0:128, :], sk, writes=[sk])
                P.op("pool", lambda e: e.tensor_copy(out=wuq0[:], in_=stage[:, 0:384]), reads=[sk], writes=["wuq0"])
                DMA(stage[0:64, 0:384], wuq_d[l, 128:192, :], sk, writes=[sk])
                P.op("pool", lambda e: e.tensor_copy(out=wuq1[0:64, :], in_=stage[0:64, 0:384]), reads=[sk], writes=["wuq1"])
                DMA(stage[:, 0:512], wukv_d[l], sk, writes=[sk])
                P.op("pool", lambda e: e.tensor_copy(out=wukv[:], in_=stage[:, 0:512]), reads=[sk], writes=["wukv"])
            units.append(uw)
            tiles = [_c_tile(tt, tt % 2) for tt in range(NT)]
            units += skew([t[0:3] for t in tiles])
            return units, skew([t[3:6] for t in tiles])

        def _c_tile(tt, p):
            tsl = slice(tt * 128, (tt + 1) * 128)
            z = zs[p]
            WA = WAs[0]

            def l1():
                def f(e):
                    ins = None
                    for c in range(8):
                        ins = e.matmul(pz[:, 0:352], lhsT=hT[:, c, tsl], rhs=WA[:, c, 0:352], start=(c == 0), stop=(c == 7))
                    return ins
                P.op("pe", f, reads=Wc(0) + [("hT", tt)], writes=["pz"])
                DMA(tab[p][:, 0:64], tabC_d[tt], ("tab", p), writes=[("tab", p)])
                P.op("dve", lambda e: e.tensor_copy(out=z[:, 0:352], in_=pz[:, 0:352]), reads=["pz"], writes=[("zs", p)])
                P.op("pool", lambda e: e.tensor_tensor(out=t1s[:, 0:320], in0=z[:, 0:320], in1=z[:, 0:320], op=ALU.mult),
                     reads=[("zs", p)], writes=[("t1",)])
                P.op("dve", lambda e: e.tensor_reduce(out=sm[p][:, 0:1], in_=t1s[:, 0:192], axis=AX.X, op=ALU.add),
                     reads=[("t1",)], writes=[("sm", p)])
                P.op("dve", lambda e: e.tensor_reduce(out=sm[p][:, 1:2], in_=t1s[:, 192:320], axis=AX.X, op=ALU.add),
                     reads=[("t1",), ("sm", p)], writes=[("sm", p)])

            def l2():
                rsqrt_small(0, 1, 1.0 / 192, p)
                rsqrt_small(1, 1, 1.0 / 128, p)
                P.op("dve", lambda e: e.scalar_tensor_tensor(out=zr[p][:, 0:192], in0=z[:, 0:192], scalar=sm2[p][:, 0:1], in1=gcq[:],
                                                             op0=ALU.mult, op1=ALU.mult),
                     reads=[("zs", p), ("sm2", p, 0), "gcq"], writes=[("zr", p)])
                P.op("dve", lambda e: e.scalar_tensor_tensor(out=zr[p][:, 192:320], in0=z[:, 192:320], scalar=sm2[p][:, 1:2], in1=gckv[:],
                                                             op0=ALU.mult, op1=ALU.mult),
                     reads=[("zs", p), ("sm2", p, 1), "gckv", ("zr", p)], writes=[("zr", p)])
                rope("pool", z[:, 320:352], 1, 1, 16, tab[p][:, 0:32], tab[p][:, 32:64], t2s[:, 0:32], t3s[:, 0:32], [("zs", p)], p)
                P.op("dve", lambda e: e.tensor_tensor(out=krp[p][:, 64:96], in0=t2s[:, 0:32], in1=t3s[:, 0:32], op=ALU.add),
                     reads=[("t2",), ("t3", 0), ("t3", 1), ("krp", p)], writes=[("krp", p)])

            def l3():
                def ftr(e):
                    e.transpose(out=ptr[:, 0:128], in_=zr[p][:, 0:128], identity=ident[:])
                    e.transpose(out=ptr[0:64, 128:256], in_=zr[p][:, 128:192], identity=ident[:])
                    e.transpose(out=ptr[:, 256:384], in_=zr[p][:, 192:320], identity=ident[:])
                    return e.transpose(out=ptr[0:96, 384:512], in_=krp[p][:, 0:96], identity=ident[:])
                P.op("pe", ftr, reads=[("zr", p), ("krp", p), "ident"], writes=["pz2"])
                P.op("dve", lambda e: e.tensor_copy(out=cl[:, 0, tsl], in_=ptr[:, 0:128]), reads=["pz2"], writes=[("cl", 0, tt)])
                P.op("dve", lambda e: e.tensor_copy(out=cl[0:64, 1, tsl], in_=ptr[0:64, 128:256]), reads=["pz2"], writes=[("cl", 1, tt)])
                P.op("dve", lambda e: e.tensor_copy(out=cl[:, 2, tsl], in_=ptr[:, 256:384]), reads=["pz2"], writes=[("cl", 2, tt)])
                P.op("dve", lambda e: e.tensor_copy(out=cl[64:96, 1, tsl], in_=ptr[64:96, 384:512]), reads=["pz2"], writes=[("cl", 3, tt)])

            def q1():
                def fq(e):
                    e.matmul(pz[:, 512:896], lhsT=cl[:, 0, tsl], rhs=wuq0[:], start=True, stop=False)
                    return e.matmul(pz[:, 512:896], lhsT=cl[0:64, 1, tsl], rhs=wuq1[0:64, :], start=False, stop=True)
                P.op("pe", fq, reads=[("cl", 0, tt), ("cl", 1, tt), "wuq0", "wuq1"], writes=["pz2"])
                DMA(tab[p][:, 0:64], tabC_d[tt], ("tab", p), writes=[("tab", p)])
                P.op("dve", lambda e: e.tensor_copy(out=z[:, 0:384], in_=pz[:, 512:896]), reads=["pz2"], writes=[("zs", p)])

            def q2():
                z3 = z[:, 0:384].rearrange("p (h r) -> p h r", h=4)
                zr3 = zr[p][:, 0:384].rearrange("p (h r) -> p h r", h=4)
                t2v = t2s[:, 0:128].rearrange("p (h r) -> p h r", h=4)
                t3v = t3s[:, 0:128].rearrange("p (h r) -> p h r", h=4)
                P.op("pool", lambda e: e.tensor_copy(out=zr3[:, :, 0:64], in_=z3[:, :, 0:64]), reads=[("zs", p)], writes=[("zr", p)])
                P.op("pool", lambda e: e.tensor_tensor(out=t2v, in0=z3[:, :, 64:96],
                                                       in1=tab[p][:, 0:32].unsqueeze(1).to_broadcast([128, 4, 32]), op=ALU.mult),
                     reads=[("zs", p), ("tab", p)], writes=[("t2",)])
                for t in range(2):
                    P.op("pool", lambda e, t=t: e.tensor_tensor(
                        out=t3v[:, :, 16 * t:16 * t + 16], in0=z3[:, :, 64 + 16 * (1 - t):80 + 16 * (1 - t)],
                        in1=tab[p][:, 32 + 16 * t:48 + 16 * t].unsqueeze(1).to_broadcast([128, 4, 16]), op=ALU.mult),
                        reads=[("zs", p), ("tab", p)], writes=[("t3", t)])
                P.op("dve", lambda e: e.tensor_tensor(out=zr3[:, :, 64:96], in0=t2v, in1=t3v, op=ALU.add),
                     reads=[("t2",), ("t3", 0), ("t3", 1), ("zr", p)], writes=[("zr", p)])

            def q3():
                def ftq(e):
                    ins = None
                    for h in range(4):
                        ins = e.transpose(out=ptr[0:96, h * 128:(h + 1) * 128], in_=zr[p][:, h * 96:(h + 1) * 96], identity=ident[:])
                    return ins
                P.op("pe", ftq, reads=[("zr", p), "ident"], writes=["pz2"])
                P.op("dve", lambda e: e.tensor_copy(out=hT[0:96, 0:4, tsl], in_=ptr[0:96, 0:512].rearrange("p (h t) -> p h t", h=4)),
                     reads=["pz2"], writes=[("qC", j, tt) for j in range(4)] + hT_cells(range(NT)))
            return [l1, l2, l3, q1, q2, q3]

        def units_Ck():
            units = []

            def ukr():
                for h in range(4):
                    P.op("dve", lambda e, h=h: e.tensor_copy(out=kc[64:96, h, :], in_=cl[64:96, 1, :]),
                         reads=[("cl", 3, t) for t in range(NT)], writes=[("kc", h, t) for t in range(NT)])
            units.append(ukr)
            for h in range(4):
                for tc in range(4):
                    def uk(h=h, tc=tc):
                        csl = slice(tc * 512, (tc + 1) * 512)
                        P.op("pe", lambda e: e.matmul(pz[0:64, 512:1024], lhsT=wukv[:, h * 128:h * 128 + 64], rhs=cl[:, 2, csl],
                                                      start=True, stop=True),
                             reads=[("cl", 2, t) for t in range(4 * tc, 4 * tc + 4)] + ["wukv"], writes=["pz2"])
                        P.op("dve", lambda e: e.tensor_copy(out=kc[0:64, h, csl], in_=pz[0:64, 512:1024]),
                             reads=["pz2"], writes=[("kcn", h, tc)] + [("kc", h, t) for t in range(4 * tc, 4 * tc + 4)])
                    units.append(uk)
            return units

        def units_Vc(h0):
            units = []
            for tt in range(NT):
                def uv(tt=tt):
                    tsl = slice(tt * 128, (tt + 1) * 128)

                    def fv(e):
                        ins = None
                        for i in range(2):
                            h = h0 + i
                            ins = e.matmul(pz[:, i * 64:(i + 1) * 64], lhsT=cl[:, 2, tsl], rhs=wukv[:, h * 128 + 64:h * 128 + 128],
                                           start=True, stop=True)
                        return ins
                    P.op("pe", fv, reads=[("cl", 2, tt), "wukv"], writes=["pz"])
                    for i in range(2):
                        h = h0 + i
                        lo = 0 if h % 2 == 0 else 64
                        P.op("dve", lambda e, i=i, h=h, lo=lo: e.tensor_copy(out=vext[:, tt, h, lo:lo + 64], in_=pz[:, i * 64:(i + 1) * 64]),
                             reads=["pz"], writes=[("vext", h)])
                units.append(uv)
            return units

        def heads_C():
            heads = []
            for h in (2, 3, 0, 1):
                hf = h % 2
                heads.append(dict(q=hT[0:96, h, :], k=kc[0:96, h, :], vh=h, ohalf=hf, scale=float(96 ** -0.5),
                                  window=False, sink=None, ot=6 + h // 2, gslot=h // 2, evac="dve",
                                  rq=[("qC", h, t) for t in range(NT)] + hT_cells(range(NT)),
                                  rk=[("kcn", h, tc) for tc in range(4)] + [("kc", h, t) for t in range(NT)]))
            return heads

        def units_WoLoad(l):
            units = []
            WO = WOs[0]
            for j in range(8):
                def u(j=j):
                    st = stages[j % 2]
                    sk = ("stage", j % 2)
                    if j < 6:
                        mixb = 384 * (j // 3)
                        jj = j % 3
                        r0 = mixb + jj * 64
                        r1 = mixb + (jj + 3) * 64
                        P.dma(lambda e, sem: (e.dma_start(out=st[0:64, :], in_=wout_d[l, r0:r0 + 64, :]).then_inc(sem, 16),
                                              e.dma_start(out=st[64:128, :], in_=wout_d[l, r1:r1 + 64, :]).then_inc(sem, 16)),
                              2, sk, writes=[sk])
                    else:
                        r0 = 768 + (j - 6) * 128
                        DMA(st[:], wout_d[l, r0:r0 + 128, :], sk, writes=[sk])
                    P.op("pool", lambda e: e.tensor_copy(out=WO[:, j, :], in_=st[:]), reads=[sk], writes=Wc(0))
                units.append(u)
            return units

        def units_O(s, l, src, dst, last):
            oc = oT_cells_all()
            tiles = [_o_tile(s, src, dst, last, oc, tt, tt % 2) for tt in range(NT)]
            return skew(tiles)

        def _o_tile(s, src, dst, last, oc, tt, p):
            tsl = slice(tt * 128, (tt + 1) * 128)
            xb, xc = xring[tt % 4]

            def s1():
                DMA(xb[:], src[s, tsl, :], xc, reads=[("xs", s, tt)], writes=[xc])
                acc = pz if p == 0 else Sp[0]
                acells = ["pz", "pz2"] if p == 0 else [("S", 0)]
                WO = WOs[0]

                def fo(e):
                    ins = None
                    for n in range(2):
                        for j in range(8):
                            ins = e.matmul(acc[:, n * 512:(n + 1) * 512], lhsT=oT[:, j, tsl], rhs=WO[:, j, n * 512:(n + 1) * 512],
                                           start=(j == 0), stop=(j == 7))
                    return ins
                P.op("pe", fo, reads=Wc(0) + oc, writes=acells)
                P.op("dve", lambda e: e.tensor_tensor(out=xb[:], in0=xb[:], in1=acc[:], op=ALU.add),
                     reads=[xc] + acells, writes=[xc])

            def sq():
                if last:
                    P.op("pool", lambda e: e.memset(sm[p][:, 0:1], 0.0), writes=[("sm", p)])
                    P.op("act", lambda e: e.activation(out=Sp[1][:], in_=xb[:], func=AF.Square, accum_out=sm[p][:, 0:1]),
                         reads=[xc, ("sm", p)], writes=[("S", 1), ("sm", p)])
                    rsqrt_small(0, 1, 1.0 / D, p)

            def s2():
                if last:
                    P.op("dve", lambda e: e.scalar_tensor_tensor(out=xb[:], in0=xb[:], scalar=sm2[p][:, 0:1], in1=gx[:],
                                                                 op0=ALU.mult, op1=ALU.mult),
                         reads=[xc, ("sm2", p, 0), "gx"], writes=[xc])
                DMA(dst[s, tsl, :], xb[:], ("xo", tt % 4), reads=[xc], writes=[("xs", s, tt)])
            return [s1, sq, s2] if last else [s1, s2]

        iters = [(s, l) for s in range(nseq) for l in range(depth)]
        for it, (s, l) in enumerate(iters):
            src = x_d if l == 0 else xs_d
            last = (l == depth - 1)
            dst = out_d if last else xs_d
            DMA(gx[:], gx_d[l], "gx", writes=["gx"])
            DMA(gqk[:], gqk_d[l], "gqk", writes=["gqk"])
            DMA(gcq[:], gcq_d[l], "gcq", writes=["gcq"])
            DMA(gckv[:], gckv_d[l], "gckv", writes=["gckv"])
            DMA(sinkt[:], sink_d[l], "sink", writes=["sink"])
            P.op("act", lambda e: e.activation(out=es_t[:], in_=sinkt[:], func=AF.Exp), reads=["sink"], writes=["es"])

            lb, tb, gb = units_projAB(s, l, 1)
            la, ta, ga = units_projAB(s, l, 0)
            zipped = []
            for i, u in enumerate(tb):
                zipped.append(u)
                if i < len(gb):
                    zipped.append(gb[i])
            zipped += gb[len(tb):]
            if it == 0:
                inter(units_N(s, l, src), lb)
                inter(zipped, la)
            else:
                inter(units_N(s, l, src) + zipped, la)
            inter(units_attention_pairs(pairs_AB(1)), ta)
            seq(ga)
            c_lat, c_q = units_projC1(s, l)
            inter(units_attention_pairs(pairs_AB(0)), c_lat + skew(gate_tiles(2, [6, 7], 0)) + c_q + units_Ck() + units_Vc(2))
            side = units_WoLoad(l)
            if it + 1 < len(iters):
                side = side + units_projAB(iters[it + 1][0], iters[it + 1][1], 1)[0]
            uc = units_attention(heads_C())
            inter(uc[:56], units_Vc(0))
            inter(uc[56:], side)
            if last:
                DMA(gx[:], gf_d, "gx", writes=["gx"])
            seq(units_O(s, l, src, dst, last))
        P.emit(final_wait_keys=[("xo", i) for i in range(4)])
    return nc


def _rope_tab(pos, dim):
    inv = (np.float32(10000.0) ** (-np.arange(0, dim, 2, dtype=np.float32) / np.float32(dim))).astype(np.float32)
    ang = pos.astype(np.float32)[:, None] * inv[None, :]
    ang = np.concatenate([ang, ang], axis=-1).astype(np.float32)
    cos = np.cos(ang).astype(np.float32)
    sin = np.sin(ang).astype(np.float32)
    half = dim // 2
    sin_signed = np.concatenate([-sin[:, :half], sin[:, half:]], axis=-1)
    return cos, sin_signed


def _constants():
    t = np.arange(S)
    cr, sr = _rope_tab(t // 64, 32)
    cc, sc = _rope_tab(t % 64, 32)
    tabA = np.concatenate([cr, cc, sr, sc], axis=-1)
    cb, sbn = _rope_tab(t, 64)
    tabB = np.concatenate([cb, sbn], axis=-1)
    ccm, scm = _rope_tab(t, 32)
    tabC = np.concatenate([ccm, scm], axis=-1)
    b = np.arange(128)[:, None, None]
    r = np.arange(4)[None, :, None]
    a = np.arange(256)[None, None, :]
    mask = np.where(np.abs((r - 1) * 128 + b - a) <= 128, 0.0, -30000.0).astype(np.float32).reshape(128, 1024)
    return dict(tabA=np.ascontiguousarray(tabA.reshape(NT, 128, 128), dtype=np.float32),
                tabB=np.ascontiguousarray(tabB.reshape(NT, 128, 128), dtype=np.float32),
                tabC=np.ascontiguousarray(tabC.reshape(NT, 128, 64), dtype=np.float32),
                maskb=mask, ident=np.eye(128, dtype=np.float32))


_NC_CACHE = {}


def kernel(x, norm_g, w_in, a_q_norm, a_k_norm, b_sink, c_q_norm, c_kv_norm, c_w_uq, c_w_ukv, w_out, final_g):
    f = lambda a: np.ascontiguousarray(np.asarray(a), dtype=np.float32)
    x = f(x); norm_g = f(norm_g); w_in = f(w_in); w_out = f(w_out)
    a_q_norm = f(a_q_norm); a_k_norm = f(a_k_norm); b_sink = f(b_sink)
    c_q_norm = f(c_q_norm); c_kv_norm = f(c_kv_norm); c_w_uq = f(c_w_uq); c_w_ukv = f(c_w_ukv); final_g = f(final_g)
    depth = norm_g.shape[0]
    rep = lambda v: np.ascontiguousarray(np.broadcast_to(v[:, None, :], (v.shape[0], 128, v.shape[1])))
    gqk = np.concatenate([np.tile(a_q_norm, (1, 6)), np.tile(a_k_norm, (1, 2))], axis=1)
    shared = dict(w_in=w_in, w_out=w_out, c_w_uq=c_w_uq, c_w_ukv=c_w_ukv,
                  gx=rep(norm_g), gf=np.ascontiguousarray(np.broadcast_to(final_g[None, :], (128, D))),
                  gqk=rep(gqk), gcq=rep(c_q_norm), gckv=rep(c_kv_norm), sink=rep(b_sink))
    shared.update(_constants())
    if "nc" not in _NC_CACHE:
        _NC_CACHE["nc"] = build_program(depth, NSEQ)
    nc = _NC_CACHE["nc"]
    ncores = 8
    in_maps = []
    for c in range(ncores):
        m = dict(shared)
        m["x"] = np.ascontiguousarray(x[c * NSEQ:(c + 1) * NSEQ])
        in_maps.append(m)
    res = run_bass_kernel_spmd(nc, in_maps, core_ids=list(range(ncores)))
    out = np.concatenate([np.asarray(r["out"]) for r in res.results], axis=0)
    return out.astype(np.float32)
```
